# Optimizing a Trainium2 kernel written in Bass

```python
import jax, jax.numpy as jnp
from jax import lax
import numpy as np

D_MODEL = 2048
BATCH = 4
SEQ = 2048
DEPTH = 4

GRID_W = 64
CTX_LEN = 256
N_MIXERS = 2
N_HEADS = 16
HEAD_DIM = D_MODEL // N_HEADS
WIN_ROWS_MAX = 8
WIN_COLS = 16
LRU_BLOCK = 256
LRU_WIDTH = -(-4 * D_MODEL // (3 * LRU_BLOCK)) * LRU_BLOCK
N_LRU_BLOCKS = LRU_WIDTH // LRU_BLOCK
CONV_WIDTH = 4
LRU_C = 8.0
D_FF = 4 * D_MODEL
N_MOD = 6
EPS = 1e-6
N_LRU_LAYERS = (DEPTH + 1) // 2
N_NA_LAYERS = DEPTH // 2

kernel_name = 'hybrid_rglru_natten_dit_block'


def _rmsnorm(x, g):
    xf = x.astype(jnp.float32)
    xf = xf * lax.rsqrt(jnp.mean(xf * xf, axis=-1, keepdims=True) + EPS)
    return xf.astype(x.dtype) * g


def _modulate(h, shift, scale):
    return h * (1.0 + scale) + shift


def _sq_relu_mlp(h, w1, w2):
    return jnp.square(jax.nn.relu(h @ w1)) @ w2


def _centred_dwconv(u, w, b):
    left = CONV_WIDTH // 2
    right = CONV_WIDTH - 1 - left
    y = lax.conv_general_dilated(u, w[:, None, :].astype(u.dtype), window_strides=(1,),
                                 padding=[(left, right)], dimension_numbers=('NWC', 'WIO', 'NWC'),
                                 feature_group_count=u.shape[-1])
    return y + b


def _block_diag(u, w, b):
    bsz, length, _ = u.shape
    ub = u.reshape(bsz, length, N_LRU_BLOCKS, LRU_BLOCK)
    return (jnp.einsum('blnj,njk->blnk', ub, w) + b).reshape(bsz, length, LRU_WIDTH)


def _rglru_coeffs(u, lam, w_a, b_a, w_x, b_x):
    uf = u.astype(jnp.float32)
    r = jax.nn.sigmoid(_block_diag(uf, w_a.astype(jnp.float32), b_a.astype(jnp.float32)))
    i_g = jax.nn.sigmoid(_block_diag(uf, w_x.astype(jnp.float32), b_x.astype(jnp.float32)))
    log_a = -LRU_C * r * jax.nn.softplus(-lam.astype(jnp.float32))
    a = jnp.exp(log_a)
    b = jnp.sqrt(-jnp.expm1(2.0 * log_a)) * (i_g * uf)
    return a, b


def _linear_scan(a, b, h0):
    def combine(l, r):
        return (l[0] * r[0], r[0] * l[1] + r[1])
    a_cum, h = lax.associative_scan(combine, (a, b), axis=1)
    return h + a_cum * h0[:, None, :]


def _flip(t, direction):
    return t[:, ::-1] if direction == 1 else t


def _rglru_mixer(h, hc, w_in, conv_w, conv_b, lam, w_a, b_a, w_x, b_x, w_out, need_ctx):
    gate, u = jnp.split(h @ w_in, 2, axis=-1)
    if need_ctx:
        gate_c, u_c = jnp.split(hc @ w_in, 2, axis=-1)
    else:
        u_c = hc @ w_in[:, LRU_WIDTH:]
    u = _centred_dwconv(u, conv_w, conv_b)
    u_c = _centred_dwconv(u_c, conv_w, conv_b)
    ys, ys_c = [], []
    for d in range(2):
        a_c, b_c = _rglru_coeffs(_flip(u_c, d), lam[d], w_a[d], b_a[d], w_x[d], b_x[d])
        h_c = _linear_scan(a_c, b_c, jnp.zeros_like(a_c[:, 0]))
        a_l, b_l = _rglru_coeffs(_flip(u, d), lam[d], w_a[d], b_a[d], w_x[d], b_x[d])
        h_l = _linear_scan(a_l, b_l, h_c[:, -1])
        ys.append(_flip(h_l, d))
        if need_ctx:
            ys_c.append(_flip(h_c, d))
    y = (ys[0] + ys[1]).astype(h.dtype)
    out = (jax.nn.gelu(gate) * y) @ w_out
    out_c = None
    if need_ctx:
        y_c = (ys_c[0] + ys_c[1]).astype(hc.dtype)
        out_c = (jax.nn.gelu(gate_c) * y_c) @ w_out
    return out, out_c


def _na_mixer(h, hc, w_qkv, rpb, w_o, need_ctx):
    bsz, length, _ = h.shape
    rows = length // GRID_W
    kh = min(WIN_ROWS_MAX, rows)
    scale = HEAD_DIM ** -0.5
    q, k, v = jnp.split(h @ w_qkv, 3, axis=-1)
    q = q.reshape(bsz, rows, GRID_W, N_HEADS, HEAD_DIM)
    k = k.reshape(bsz, rows, GRID_W, N_HEADS, HEAD_DIM)
    v = v.reshape(bsz, rows, GRID_W, N_HEADS, HEAD_DIM)
    if need_ctx:
        qc, kc, vc = jnp.split(hc @ w_qkv, 3, axis=-1)
        qc = qc.reshape(bsz, CTX_LEN, N_HEADS, HEAD_DIM)
    else:
        kc, vc = jnp.split(hc @ w_qkv[:, D_MODEL:], 2, axis=-1)
    kc = kc.reshape(bsz, CTX_LEN, N_HEADS, HEAD_DIM)
    vc = vc.reshape(bsz, CTX_LEN, N_HEADS, HEAD_DIM)

    cols = jnp.arange(GRID_W)
    col_start = jnp.clip(cols - WIN_COLS // 2, 0, GRID_W - WIN_COLS)
    col_idx = col_start[:, None] + jnp.arange(WIN_COLS)[None, :]
    col_off = col_idx - cols[:, None] + (WIN_COLS - 1)
    bias_cols = rpb[:, :, col_off].astype(jnp.float32)

    def row_block(r):
        rs = jnp.clip(r - kh // 2, 0, rows - kh)
        q_r = lax.dynamic_index_in_dim(q, r, axis=1, keepdims=False)
        k_win = lax.dynamic_slice_in_dim(k, rs, kh, axis=1)[:, :, col_idx]
        v_win = lax.dynamic_slice_in_dim(v, rs, kh, axis=1)[:, :, col_idx]
        row_off = rs + jnp.arange(kh) - r + (WIN_ROWS_MAX - 1)
        bias = jnp.transpose(bias_cols[:, row_off], (0, 2, 1, 3))
        s_loc = jnp.einsum('bwhd,bawkhd->bhwak', q_r, k_win).astype(jnp.float32) * scale + bias[None]
        s_ctx = jnp.einsum('bwhd,bchd->bhwc', q_r, kc).astype(jnp.float32) * scale
        s = jnp.concatenate([s_loc.reshape(bsz, N_HEADS, GRID_W, kh * WIN_COLS), s_ctx], axis=-1)
        p = jax.nn.softmax(s, axis=-1).astype(v.dtype)
        p_loc = p[..., :kh * WIN_COLS].reshape(bsz, N_HEADS, GRID_W, kh, WIN_COLS)
        p_ctx = p[..., kh * WIN_COLS:]
        return (jnp.einsum('bhwak,bawkhd->bwhd', p_loc, v_win)
                + jnp.einsum('bhwc,bchd->bwhd', p_ctx, vc))

    o = lax.map(row_block, jnp.arange(rows))
    o = jnp.transpose(o, (1, 0, 2, 3, 4)).reshape(bsz, length, D_MODEL)
    out = o @ w_o
    out_c = None
    if need_ctx:
        s_c = jnp.einsum('bqhd,bkhd->bhqk', qc, kc).astype(jnp.float32) * scale
        p_c = jax.nn.softmax(s_c, axis=-1).astype(vc.dtype)
        o_c = jnp.einsum('bhqk,bkhd->bqhd', p_c, vc).reshape(bsz, CTX_LEN, D_MODEL)
        out_c = o_c @ w_o
    return out, out_c


def setup_inputs(seed: int = 0) -> dict:
    key = jax.random.key(seed)
    ks = jax.random.split(key, 24)
    f32 = jnp.float32
    nrm = lambda k, shape, s: jax.random.normal(k, shape, f32) * s
    u = jax.random.uniform(ks[12], (N_LRU_LAYERS, 2, LRU_WIDTH), f32, minval=0.9, maxval=0.999)
    s_lam = u ** (1.0 / LRU_C)
    lru_lambda = jnp.log(s_lam) - jnp.log1p(-s_lam)
    return {
        'x': nrm(ks[0], (BATCH, SEQ, D_MODEL), 1.0),
        'c': nrm(ks[1], (BATCH, D_MODEL), 1.0),
        'ctx': nrm(ks[2], (BATCH, CTX_LEN, D_MODEL), 1.0),
        'c_ctx': nrm(ks[3], (D_MODEL,), 1.0),
        'ada_w': nrm(ks[4], (DEPTH, D_MODEL, N_MOD * D_MODEL), 0.5 * D_MODEL ** -0.5),
        'ada_b': nrm(ks[5], (DEPTH, N_MOD * D_MODEL), 0.02),
        'norm1_g': 1.0 + nrm(ks[6], (DEPTH, D_MODEL), 0.02),
        'norm2_g': 1.0 + nrm(ks[7], (DEPTH, D_MODEL), 0.02),
        'mlp_w1': nrm(ks[8], (DEPTH, D_MODEL, D_FF), D_MODEL ** -0.5),
        'mlp_w2': nrm(ks[9], (DEPTH, D_FF, D_MODEL), D_FF ** -0.5),
        'lru_w_in': nrm(ks[10], (N_LRU_LAYERS, D_MODEL, 2 * LRU_WIDTH), D_MODEL ** -0.5),
        'lru_conv_w': nrm(ks[11], (N_LRU_LAYERS, CONV_WIDTH, LRU_WIDTH), CONV_WIDTH ** -0.5),
        'lru_conv_b': nrm(ks[13], (N_LRU_LAYERS, LRU_WIDTH), 0.02),
        'lru_lambda': lru_lambda,
        'lru_wa': nrm(ks[14], (N_LRU_LAYERS, 2, N_LRU_BLOCKS, LRU_BLOCK, LRU_BLOCK), LRU_BLOCK ** -0.5),
        'lru_ba': nrm(ks[15], (N_LRU_LAYERS, 2, N_LRU_BLOCKS, LRU_BLOCK), 0.02),
        'lru_wx': nrm(ks[16], (N_LRU_LAYERS, 2, N_LRU_BLOCKS, LRU_BLOCK, LRU_BLOCK), LRU_BLOCK ** -0.5),
        'lru_bx': nrm(ks[17], (N_LRU_LAYERS, 2, N_LRU_BLOCKS, LRU_BLOCK), 0.02),
        'lru_w_out': nrm(ks[18], (N_LRU_LAYERS, LRU_WIDTH, D_MODEL), LRU_WIDTH ** -0.5),
        'na_w_qkv': nrm(ks[19], (N_NA_LAYERS, D_MODEL, 3 * D_MODEL), D_MODEL ** -0.5),
        'na_rpb': nrm(ks[20], (N_NA_LAYERS, N_HEADS, 2 * WIN_ROWS_MAX - 1, 2 * WIN_COLS - 1), 0.1),
        'na_w_o': nrm(ks[21], (N_NA_LAYERS, D_MODEL, D_MODEL), D_MODEL ** -0.5),
        'final_g': 1.0 + nrm(ks[22], (D_MODEL,), 0.02),
    }


def reference(x, c, ctx, c_ctx, ada_w, ada_b, norm1_g, norm2_g, mlp_w1, mlp_w2,
              lru_w_in, lru_conv_w, lru_conv_b, lru_lambda, lru_wa, lru_ba, lru_wx, lru_bx, lru_w_out,
              na_w_qkv, na_rpb, na_w_o, final_g):
    xc = ctx
    silu_c = jax.nn.silu(c)
    silu_cc = jax.nn.silu(c_ctx)
    for i in range(DEPTH):
        need_ctx = i < DEPTH - 1
        mods = silu_c @ ada_w[i] + ada_b[i]
        sh1, sc1, gt1, sh2, sc2, gt2 = [m[:, None, :] for m in jnp.split(mods, N_MOD, axis=-1)]
        if need_ctx:
            csh1, csc1, cgt1, csh2, csc2, cgt2 = jnp.split(silu_cc @ ada_w[i] + ada_b[i], N_MOD, axis=-1)
        else:
            csh1, csc1 = jnp.split(silu_cc @ ada_w[i][:, :2 * D_MODEL] + ada_b[i][:2 * D_MODEL], 2, axis=-1)
        h = _modulate(_rmsnorm(x, norm1_g[i]), sh1, sc1)
        hc = _modulate(_rmsnorm(xc, norm1_g[i]), csh1, csc1)
        j = i // N_MIXERS
        if i % N_MIXERS == 0:
            out, out_c = _rglru_mixer(h, hc, lru_w_in[j], lru_conv_w[j], lru_conv_b[j], lru_lambda[j],
                                      lru_wa[j], lru_ba[j], lru_wx[j], lru_bx[j], lru_w_out[j], need_ctx)
        else:
            out, out_c = _na_mixer(h, hc, na_w_qkv[j], na_rpb[j], na_w_o[j], need_ctx)
        x = x + gt1 * out
        h2 = _modulate(_rmsnorm(x, norm2_g[i]), sh2, sc2)
        x = x + gt2 * _sq_relu_mlp(h2, mlp_w1[i], mlp_w2[i])
        if need_ctx:
            xc = xc + cgt1 * out_c
            hc2 = _modulate(_rmsnorm(xc, norm2_g[i]), csh2, csc2)
            xc = xc + cgt2 * _sq_relu_mlp(hc2, mlp_w1[i], mlp_w2[i])
    return _rmsnorm(x, final_g)
```

```python
import numpy as np
import concourse.bass as bass
import concourse.mybir as mybir
from concourse.bass_utils import run_bass_kernel_spmd

F32 = mybir.dt.float32
BF16 = mybir.dt.bfloat16
AF = mybir.ActivationFunctionType
ALU = mybir.AluOpType
AX = mybir.AxisListType

D = 2048
DC = 16
NTOK = 2304
CTX = 256
SEQ = 2048
DEPTH = 4
W_LRU = 2816
LC = 22
NBLK = 11
DFF = 8192
NH = 16
EPS = 1e-6
TILES = [(0, 256), (256, 768), (768, 1280), (1280, 1792), (1792, 2304)]
HT = 1152
LT = [(0, 256), (256, 768), (768, 1152)]
NBL = 6
NHL = 8
PAIRS = [[0, 1], [2, 3], [4, 5], [6, 7]]
NEG = -30000.0
NKEY = 896


class Sched:
    ENG = ["pe", "act", "dve", "pool", "sp"]

    def __init__(self):
        self.ops = []
        self.by_eng = {e: [] for e in self.ENG}
        self.wr = {}
        self.rd = {}
        self.fence_ops = []
        self.dma_count = {"pool": 0, "sp": 0, "act": 0}
        self.dma_last = {}
        self.ndma = {"pool": 20, "sp": 20, "act": 4}
        self.last_cc = None

    def add(self, eng, fn, r=(), w=(), kind="c"):
        i = len(self.ops)
        op = dict(id=i, eng=eng, fn=fn, kind=kind, deps={}, users=False)
        rkey = eng if kind == "c" else ("dma", i)
        for k in r:
            if k in self.wr:
                op["deps"][self.wr[k]] = "raw"
            self.rd.setdefault(k, {})[rkey] = i
        for k in w:
            if k in self.wr:
                op["deps"].setdefault(self.wr[k], "waw")
            for j in self.rd.get(k, {}).values():
                if j != i:
                    op["deps"].setdefault(j, "war")
            self.wr[k] = i
            self.rd[k] = {}
        for j in self.fence_ops:
            op["deps"].setdefault(j, "raw")
        if kind == "dma":
            n = self.dma_count[eng]
            self.dma_count[eng] = n + 1
            slot = (eng, n % self.ndma[eng])
            op["dslot"] = slot
            op["dval"] = 16 * (n // self.ndma[eng] + 1)
            if slot in self.dma_last:
                op["deps"].setdefault(self.dma_last[slot], "raw")
            self.dma_last[slot] = i
        if kind == "cc":
            self.last_cc = i
        self.ops.append(op)
        self.by_eng[eng].append(i)
        return i

    def fence(self):
        last = []
        for e in self.ENG:
            if self.by_eng[e]:
                last.append(self.by_eng[e][-1])
        last += list(self.dma_last.values())
        if self.last_cc is not None:
            last.append(self.last_cc)
        self.fence_ops = sorted(set(last))

    def wait_all(self, eng, ids):
        i = self.add(eng, None)
        for j in ids:
            self.ops[i]["deps"][j] = "raw"
        return i

    def emit(self, nc):
        ops = self.ops
        for op in ops:
            nd = {}
            for j, t in op["deps"].items():
                p = ops[j]
                if p["kind"] == "c" and p["eng"] == op["eng"]:
                    if op["eng"] == "pe" or t != "raw":
                        continue
                nd[j] = t
            op["deps"] = nd
            for j in nd:
                ops[j]["users"] = True
        cnt = {e: 0 for e in self.ENG}
        ncc = 0
        for op in ops:
            if op["kind"] == "c" and op["users"] and op["fn"] is not None:
                cnt[op["eng"]] += 1
                op["cval"] = cnt[op["eng"]]
            if op["kind"] == "cc":
                ncc += 1
                op["ccval"] = ncc
        import contextlib
        st = contextlib.ExitStack()
        sem_e = {e: st.enter_context(nc.semaphore("prog_" + e)) for e in self.ENG}
        sem_cc = st.enter_context(nc.semaphore("prog_cc"))
        sem_d = {}
        for q, n in self.ndma.items():
            for k in range(n):
                sem_d[(q, k)] = st.enter_context(nc.semaphore("dma_%s_%d" % (q, k)))
        plans = {}
        for e in self.ENG:
            seen_c = {f: 0 for f in self.ENG}
            seen_d = {}
            seen_cc = 0
            plan = []
            for i in self.by_eng[e]:
                op = ops[i]
                need_c = {}
                need_d = {}
                need_cc = 0
                for j in op["deps"]:
                    p = ops[j]
                    if p["kind"] == "cc":
                        need_cc = max(need_cc, p["ccval"])
                    elif p["kind"] == "c":
                        if p["fn"] is None:
                            continue
                        need_c[p["eng"]] = max(need_c.get(p["eng"], 0), p["cval"])
                    else:
                        need_d[p["dslot"]] = max(need_d.get(p["dslot"], 0), p["dval"])
                waits = []
                if need_cc > seen_cc:
                    seen_cc = need_cc
                    waits.append((sem_cc, need_cc))
                for f, v in need_c.items():
                    if v > seen_c[f]:
                        seen_c[f] = v
                        waits.append((sem_e[f], v))
                for sl, v in need_d.items():
                    if v > seen_d.get(sl, 0):
                        seen_d[sl] = v
                        waits.append((sem_d[sl], v))
                plan.append((op, waits))
            plans[e] = plan
        self.plans = plans
        self.sem_names = {id(v): k for k, v in list(sem_e.items()) + list(sem_d.items()) + [("cc", sem_cc)]}
        self.stats = {e: len(plans[e]) for e in self.ENG}
        self.stats["incs"] = dict(cnt)

        def run(e, eng):
            for op, waits in plans[e]:
                for s, v in waits:
                    eng.wait_ge(s, v)
                if op["fn"] is None:
                    continue
                ins = op["fn"](eng)
                if op["kind"] == "dma":
                    ins.then_inc(sem_d[op["dslot"]], 16)
                elif op["kind"] == "cc":
                    ins.then_inc(sem_cc, 1)
                elif op["users"]:
                    ins.then_inc(sem_e[e], 1)

        with st:
            with nc.Block() as block:
                @block.tensor
                def _(eng):
                    run("pe", eng)

                @block.scalar
                def _(eng):
                    run("act", eng)

                @block.vector
                def _(eng):
                    run("dve", eng)

                @block.gpsimd
                def _(eng):
                    run("pool", eng)

                @block.sync
                def _(eng):
                    run("sp", eng)


PIPE_NA = True
OVERLAP_MODS = True
CC_SEM = False


def build_program(n_layers=DEPTH, debug=False):
    nc = bass.Bass("TRN2", target_bir_lowering=False)
    S = Sched()

    def din(name, shape, dt=F32):
        return nc.dram_tensor(name, list(shape), dt, kind="ExternalInput").ap()

    xT = din("xT", [D, HT])
    cv = din("cv", [128, DC, 2])
    flg = din("flags", [128, 2])
    ident_d = din("ident", [128, 128])
    ada_w = din("ada_w", [DEPTH, D, 6 * D])
    ada_b = din("ada_b", [128, DEPTH, 96])
    n1g = din("n1g", [128, DEPTH, DC])
    n2g = din("n2g", [128, DEPTH, DC])
    fing = din("fing", [128, DC])
    w1d = din("mlp_w1", [DEPTH, D, DFF])
    w2d = din("mlp_w2", [DEPTH, DFF, D])
    LCL = 2 * NBL
    lwin_g = din("lru_w_g", [2, D, NBL * 256])
    lwin_u = din("lru_w_u", [2, D, NBL * 256])
    lcw = din("lru_cw", [128, 2, LCL, 4])
    lcb = din("lru_cb", [128, 2, LCL])
    llam = din("lru_lam", [128, 2, 2, LCL])
    lwa = din("lru_wa", [2, 2, NBL, 256, 256])
    lba = din("lru_ba", [128, 2, 2, LCL])
    lwx = din("lru_wx", [2, 2, NBL, 256, 256])
    lbx = din("lru_bx", [128, 2, 2, LCL])
    lwout = din("lru_w_out", [2, 2 * NBL * 256, D])
    nwq = din("na_w_q", [2, D, NHL * 128])
    nwk = din("na_w_k", [2, D, NHL * 128])
    nwv = din("na_w_v", [2, D, NHL * 128])
    nwo = din("na_w_o", [2, D, D])
    nbias = din("na_bias", [2, NHL, 128, 5 * NKEY])
    outT = nc.dram_tensor("outT", [D, HT], F32, kind="ExternalOutput").ap()
    kw = dict(kind="ExternalOutput") if debug else {}
    CZ = NBL * 256
    Xm = nc.dram_tensor("Xm", [D, HT], F32, **kw).ap()
    Hs = nc.dram_tensor("Hs", [D, HT], BF16).ap()
    GQ = 512
    Hg = [nc.dram_tensor("Hg%d" % q, [2 * GQ, HT], BF16, **kw).ap() for q in range(D // GQ)]
    Zs = [nc.dram_tensor("Zs%d" % h, [CZ, HT], BF16).ap() for h in range(2)]
    Zg = [[nc.dram_tensor("Zg%d_%d" % (h, q), [2 * GQ, HT], BF16, **kw).ap() for q in range(CZ // GQ)] for h in range(2)]

    SB_TOTAL = 224 * 1024
    SB_BASE = ((SB_TOTAL - nc.sbuf_bytes_remaining + 63) // 64) * 64
    cap = SB_TOTAL - 2048
    names = [0]

    def esz(dt):
        return 4 if dt == F32 else 2

    def alloc_at(off, shape, dt):
        names[0] += 1
        n = 1
        for s_ in shape[1:]:
            n *= s_
        nbytes = n * esz(dt)
        assert off + nbytes <= cap, (off, nbytes, cap)
        t = nc.alloc_sbuf_tensor_at("t%d" % names[0], list(shape), dt, offset=off)
        return t, off + ((nbytes + 63) // 64) * 64

    off = SB_BASE
    ones, off = alloc_at(off, [128, 128], BF16)
    ident, off = alloc_at(off, [128, 128], BF16)
    SV, off = alloc_at(off, [128, DC, 2], BF16)
    CVs, off = alloc_at(off, [128, DC, 2], F32)
    FLG, off = alloc_at(off, [128, 2], F32)
    MODSB, MAB = [], []
    for _ in range(2):
        t_, off = alloc_at(off, [128, 96, 3], F32); MODSB.append(t_)
        t_, off = alloc_at(off, [128, 2, DC, 3], F32); MAB.append(t_)
    ADAB, off = alloc_at(off, [128, DEPTH, 96], F32)
    N1G, off = alloc_at(off, [128, DEPTH, DC], F32)
    N2G, off = alloc_at(off, [128, DEPTH, DC], F32)
    FING, off = alloc_at(off, [128, DC], F32)
    LCW, off = alloc_at(off, [128, 2, LCL, 4], F32)
    LCB, off = alloc_at(off, [128, 2, LCL], F32)
    LLAM, off = alloc_at(off, [128, 2, 2, LCL], F32)
    LSA, off = alloc_at(off, [128, 2, 2, LCL], F32)
    LSAH, off = alloc_at(off, [128, 2, 2, LCL], F32)
    LBA, off = alloc_at(off, [128, 2, 2, LCL], F32)
    LBX, off = alloc_at(off, [128, 2, 2, LCL], F32)
    NRING = 6
    RING = []
    for _ in range(NRING):
        t, off = alloc_at(off, [128, 4096], BF16)
        RING.append(t)
    DYN = off
    ring_ctr = [0]

    def ring_next():
        k = ring_ctr[0] % NRING
        ring_ctr[0] += 1
        return k

    pa = nc.alloc_psum_tensor("pa", [128, 1536], F32)
    ptb = nc.alloc_psum_tensor("ptb", [128, 1024], BF16)
    sA = nc.alloc_psum_tensor("sA", [128, 1024], F32)
    sB = nc.alloc_psum_tensor("sB", [128, 1024], F32)
    GB = [(pa, 0, 0), (pa, 512, 1), (pa, 1024, 2), (sA, 0, 4), (sA, 512, 5), (sB, 0, 6), (sB, 512, 7)]
    gb_ctr = [0]

    def bank(n=None):
        lst = GB if n is None else GB[:n]
        t, o, k = lst[gb_ctr[0] % len(lst)]
        gb_ctr[0] += 1
        return (lambda w, t=t, o=o: t[:, o:o + w]), ("ps", k)

    OUT_IDS = []

    def dma(q, out, in_, r=(), w=()):
        return S.add(q, lambda e: e.dma_start(out=out, in_=in_), r=r, w=w, kind="dma")

    def mm(out, lhsT, rhs, start, stop, r=(), w=()):
        return S.add("pe", lambda e: e.matmul(out, lhsT, rhs, start=start, stop=stop), r=r, w=w)

    def act(out, in_, func, bias=0.0, scale=1.0, r=(), w=(), accum=None):
        if accum is None:
            return S.add("act", lambda e: e.activation(out, in_, func, bias=bias, scale=scale), r=r, w=w)
        return S.add("act", lambda e: e.activation(out, in_, func, bias=bias, scale=scale, accum_out=accum), r=r, w=w)

    def dve(fn, r=(), w=()):
        return S.add("dve", fn, r=r, w=w)

    def load_w(dst_ap, src_ap, slot, extra_w=()):
        return dma("pool", dst_ap, src_ap, w=[("ring", slot)] + list(extra_w))

    def allgather(src, dst, r, w):
        return S.add("pool", lambda e: e.collective_compute("AllGather", ALU.bypass, ins=[src], outs=[dst], replica_groups=PAIRS),
                     r=r, w=w, kind=("cc" if CC_SEM else "c"))

    dve(lambda e: e.memset(ones[:], 1.0), w=["ones"])
    dma("pool", ident[:], ident_d, w=["ident"])
    dma("sp", CVs[:], cv, w=["cvs"])
    dma("sp", FLG[:], flg, w=["flg"])
    dma("sp", ADAB[:], ada_b, w=["adab"])
    dma("sp", N1G[:], n1g, w=["n1g"])
    dma("sp", N2G[:], n2g, w=["n2g"])
    dma("sp", FING[:], fing, w=["fing"])
    dma("sp", LCW[:], lcw, w=["lcw"])
    dma("sp", LCB[:], lcb, w=["lcb"])
    dma("sp", LLAM[:], llam, w=["llam"])
    dma("sp", LBA[:], lba, w=["lba"])
    dma("sp", LBX[:], lbx, w=["lbx"])
    act(SV[:], CVs[:], AF.Silu, r=["cvs"], w=["sv"])
    dve(lambda e: e.tensor_scalar(LBA[:], LBA[:], 0.5, None, ALU.mult), r=["lba"], w=["lba"])
    dve(lambda e: e.tensor_scalar(LBX[:], LBX[:], 0.5, None, ALU.mult), r=["lbx"], w=["lbx"])
    act(LSA[:], LLAM[:], AF.Exp, scale=-1.0, r=["llam"], w=["lsa"])
    act(LSA[:], LSA[:], AF.Ln, bias=1.0, r=["lsa"], w=["lsa"])
    dve(lambda e: e.tensor_scalar(LSAH[:], LSA[:], -4.0, None, ALU.mult), r=["lsa"], w=["lsah"])
    dve(lambda e: e.tensor_scalar(LSA[:], LSA[:], -8.0, None, ALU.mult), r=["lsa", "lsah"], w=["lsa"])


    def mods_steps(i):
        MODS, MA = MODSB[i % 2], MAB[i % 2]
        mk, ak = ("mods", i % 2), ("ma", i % 2)

        def piece(pc):
            slot = ring_next()
            wt = RING[slot][:, 0:4096].rearrange("p (k n) -> p k n", k=DC)
            load_w(wt, ada_w[i, :, pc * 256:(pc + 1) * 256].rearrange("(k p) n -> p k n", p=128), slot)
            for f in range(2):
                fc = pc * 2 + f
                pb, pk = bank(3)
                for k in range(DC):
                    mm(pb(2), wt[:, k, f * 128:(f + 1) * 128], SV[:, k, :], k == 0, k == DC - 1,
                       r=[("ring", slot), "sv"], w=[pk])
                dve(lambda e, pb=pb, fc=fc: e.tensor_scalar(MODS[:, fc, 0:2], pb(2), ADAB[:, i, fc:fc + 1], None, ALU.add),
                    r=[pk, "adab"], w=[mk])

        def derived():
            dve(lambda e: e.tensor_scalar(MODS[:, :, 2], MODS[:, :, 1], FLG[:, 0:1], None, ALU.mult), r=[mk, "flg"], w=[mk])
            dve(lambda e: e.scalar_tensor_tensor(MODS[:, :, 2], MODS[:, :, 0], FLG[:, 1:2], MODS[:, :, 2], ALU.mult, ALU.add),
                r=[mk, "flg"], w=[mk])
            for n, (gsrc, base) in enumerate([(N1G, 16), (N2G, 64)]):
                for s_ in range(3):
                    dve(lambda e, n=n, s_=s_, base=base: e.tensor_scalar(MA[:, n, :, s_], MODS[:, base:base + 16, s_], 1.0, None, ALU.add),
                        r=[mk], w=[ak])
                    dve(lambda e, n=n, s_=s_, gsrc=gsrc: e.tensor_tensor(MA[:, n, :, s_], MA[:, n, :, s_], gsrc[:, i, :], ALU.mult),
                        r=[ak, "n1g", "n2g"], w=[ak])
        return [(lambda pc=pc: piece(pc)) for pc in range(48)] + [derived]

    def mods_phase(i):
        S.fence()
        for st_ in mods_steps(i):
            st_()

    PENDING = []

    def drain(n):
        for _ in range(min(n, len(PENDING))):
            PENDING.pop(0)()

    def token_phase(i, final):
        S.fence()
        MODS, MA = MODSB[i % 2], MAB[i % 2]
        MODSN, MAN = MODSB[(i + 1) % 2], MAB[(i + 1) % 2]
        o_ = DYN
        ACC, o_ = alloc_at(o_, [128, DC, HT], F32)
        h2_off = o_
        H2, o_ = alloc_at(o_, [128, DC, HT], BF16)
        ZT, _ = alloc_at(h2_off, [128, 24 * 512], BF16)
        ZT2, o_ = alloc_at(o_, [128, 12 * 512], BF16)
        AB = []
        for _ in range(2):
            t, o_ = alloc_at(o_, [128, 4, 512], BF16)
            AB.append(t)
        RL, SQ, TMP = [], [], []
        for _ in range(2):
            t, o_ = alloc_at(o_, [128, 512], F32); RL.append(t)
            t, o_ = alloc_at(o_, [128, 512], BF16); SQ.append(t)
            t, o_ = alloc_at(o_, [128, 512], F32); TMP.append(t)
        RS, o_ = alloc_at(o_, [128, 512], F32)
        RSTD, o_ = alloc_at(o_, [128, 512], F32)
        allh2 = [("h2", c, t0) for c in range(DC) for (t0, _) in LT]
        ctr = [0]
        SS = lambda t: 2 if t == 0 else 0

        def rstd_of(t0, w):
            pb, pk = bank()
            for c in range(DC):
                q = ctr[0] % 2
                ctr[0] += 1
                act(SQ[q][:, :w], ACC[:, c, t0:t0 + w], AF.Square, r=[("acc", c, t0)], w=[("sq", q)])
                mm(pb(w), ones[:], SQ[q][:, :w], c == 0, c == DC - 1, r=[("sq", q), "ones"], w=[pk])
            act(RS[:, :w], pb(w), AF.Sqrt, bias=EPS, scale=1.0 / D, r=[pk], w=["rs"])
            dve(lambda e: e.reciprocal(RSTD[:, :w], RS[:, :w]), r=["rs"], w=["rstd"])

        def norm(t0, w, Aap, Bap, keys_r):
            rstd_of(t0, w)
            for c in range(DC):
                q = ctr[0] % 2
                ctr[0] += 1
                dve(lambda e, c=c, q=q: e.tensor_tensor(TMP[q][:, :w], ACC[:, c, t0:t0 + w], RSTD[:, :w], ALU.mult),
                    r=[("acc", c, t0), "rstd"], w=[("tmp", q)])
                act(H2[:, c, t0:t0 + w], TMP[q][:, :w], AF.Identity, bias=Bap(c), scale=Aap(c),
                    r=[("tmp", q)] + keys_r, w=[("h2", c, t0), "zt"])

        for t, (t0, t1) in enumerate(LT):
            src = (xT if i < 0 else Xm)[:, t0:t1].rearrange("(c p) n -> p c n", p=128)
            dma("sp", ACC[:, :, t0:t1], src, r=["xm"], w=[("acc", c, t0) for c in range(DC)])
        if i >= 0:
            lru = (i % 2 == 0)
            j = i // 2
            KZ = 2 * LCL if lru else DC
            KH = KZ // 2
            wsrc = lwout[j] if lru else nwo[j]
            for t, (t0, t1) in enumerate(LT):
                w = t1 - t0
                s_ = SS(t)
                zt = ZT[:, 0:KZ * 512].rearrange("p (k n) -> p k n", k=KZ)
                zt2 = ZT2[:, 0:KH * 512].rearrange("p (k n) -> p k n", k=KH)
                for rk in range(2):
                    ztr = zt[:, rk * KH:(rk + 1) * KH, :w]
                    for q in range(KH // 4):
                        dma("sp", zt[:, rk * KH + 4 * q: rk * KH + 4 * q + 4, :w],
                            Zg[0][q][rk * GQ:(rk + 1) * GQ, t0:t1].rearrange("(k p) n -> p k n", p=128),
                            r=[("zg", 0, q)], w=["zt"] + allh2)
                        dma("sp", zt2[:, 4 * q: 4 * q + 4, :w],
                            Zg[1][q][rk * GQ:(rk + 1) * GQ, t0:t1].rearrange("(k p) n -> p k n", p=128),
                            r=[("zg", 1, q)], w=["zt2"])
                    dve(lambda e, ztr=ztr: e.tensor_scalar(ztr, ztr, FLG[:, 0:1], None, ALU.mult), r=["zt", "flg"], w=["zt"])
                    dve(lambda e, ztr=ztr, zt2=zt2, w=w: e.scalar_tensor_tensor(ztr, zt2[:, :, :w], FLG[:, 1:2], ztr, ALU.mult, ALU.add),
                        r=["zt", "zt2", "flg"], w=["zt"])
                for o in range(DC):
                    slot = ring_next()
                    wt = RING[slot][:, 0:KZ * 128].rearrange("p (k n) -> p k n", k=KZ)
                    load_w(wt, wsrc[:, o * 128:(o + 1) * 128].rearrange("(k p) n -> p k n", p=128), slot)
                    pb, pk = bank()
                    for k in range(KZ):
                        mm(pb(w), wt[:, k, :], zt[:, k, :w], k == 0, k == KZ - 1, r=[("ring", slot), "zt"], w=[pk])
                    dve(lambda e, pb=pb, o=o, t0=t0, w=w, s_=s_: e.scalar_tensor_tensor(
                        ACC[:, o, t0:t0 + w], pb(w), MODS[:, 32 + o, s_:s_ + 1], ACC[:, o, t0:t0 + w], ALU.mult, ALU.add),
                        r=[pk, ("mods", i % 2), ("acc", o, t0)], w=[("acc", o, t0)])
            for t, (t0, t1) in enumerate(LT):
                s_ = SS(t)
                norm(t0, t1 - t0, lambda c, s_=s_: MA[:, 1, c, s_:s_ + 1], lambda c, s_=s_: MODS[:, 48 + c, s_:s_ + 1],
                     [("ma", i % 2), ("mods", i % 2)])
            for g in range(DFF // 512):
                sl1 = [ring_next(), ring_next()]
                w1t = []
                for u in range(2):
                    wt = RING[sl1[u]][:, :].rearrange("p (k n) -> p k n", k=DC)
                    load_w(wt, w1d[i, :, g * 512 + u * 256: g * 512 + (u + 1) * 256].rearrange("(k p) n -> p k n", p=128), sl1[u])
                    w1t.append(wt)
                sl2 = [ring_next(), ring_next()]
                w2t = []
                for u in range(2):
                    wt = RING[sl2[u]][:, :].rearrange("p (k n) -> p k n", k=2)
                    load_w(wt, w2d[i, g * 512 + u * 256: g * 512 + (u + 1) * 256, :].rearrange("(k p) n -> p k n", p=128), sl2[u])
                    w2t.append(wt)
                for t, (t0, t1) in enumerate(LT):
                    w = t1 - t0
                    s_ = SS(t)
                    ab = ctr[0] % 2
                    A = AB[ab]
                    for hc in range(4):
                        pb, pk = bank()
                        wt = w1t[hc // 2]
                        cs = (hc % 2) * 128
                        for k in range(DC):
                            mm(pb(w), wt[:, k, cs:cs + 128], H2[:, k, t0:t0 + w], k == 0, k == DC - 1,
                               r=[("ring", sl1[hc // 2]), ("h2", k, t0)], w=[pk])
                        q = ctr[0] % 2
                        ctr[0] += 1
                        act(RL[q][:, :w], pb(w), AF.Relu, r=[pk], w=[("rl", q)])
                        dve(lambda e, A=A, hc=hc, q=q, w=w: e.tensor_tensor(A[:, hc, :w], RL[q][:, :w], RL[q][:, :w], ALU.mult),
                            r=[("rl", q)], w=[("ab", ab, hc)])
                    if ctr[0] % 2 == ab:
                        ctr[0] += 1
                    for o in range(DC):
                        pb, pk = bank()
                        for hc in range(4):
                            mm(pb(w), w2t[hc // 2][:, hc % 2, o * 128:(o + 1) * 128], A[:, hc, :w], hc == 0, hc == 3,
                               r=[("ring", sl2[hc // 2]), ("ab", ab, hc)], w=[pk])
                        dve(lambda e, pb=pb, o=o, t0=t0, w=w, s_=s_: e.scalar_tensor_tensor(
                            ACC[:, o, t0:t0 + w], pb(w), MODS[:, 80 + o, s_:s_ + 1], ACC[:, o, t0:t0 + w], ALU.mult, ALU.add),
                            r=[pk, ("mods", i % 2), ("acc", o, t0)], w=[("acc", o, t0)])
        for t, (t0, t1) in enumerate(LT):
            w = t1 - t0
            s_ = SS(t)
            if final:
                rstd_of(t0, w)
                for c in range(DC):
                    dve(lambda e, c=c, t0=t0, w=w: e.tensor_tensor(ACC[:, c, t0:t0 + w], ACC[:, c, t0:t0 + w], RSTD[:, :w], ALU.mult),
                        r=[("acc", c, t0), "rstd"], w=[("acc", c, t0)])
                    act(ACC[:, c, t0:t0 + w], ACC[:, c, t0:t0 + w], AF.Identity, scale=FING[:, c:c + 1],
                        r=[("acc", c, t0), "fing"], w=[("acc", c, t0)])
                OUT_IDS.append(dma("sp", outT[:, t0:t1].rearrange("(c p) n -> p c n", p=128), ACC[:, :, t0:t1],
                                   r=[("acc", c, t0) for c in range(DC)], w=[("out", t)]))
            else:
                dma("sp", Xm[:, t0:t1].rearrange("(c p) n -> p c n", p=128), ACC[:, :, t0:t1],
                    r=[("acc", c, t0) for c in range(DC)], w=["xm"])
                norm(t0, w, lambda c, s_=s_: MAN[:, 0, c, s_:s_ + 1], lambda c, s_=s_: MODSN[:, c, s_:s_ + 1],
                     [("ma", (i + 1) % 2), ("mods", (i + 1) % 2)])
                dma("sp", Hs[:, t0:t1].rearrange("(c p) n -> p c n", p=128), H2[:, :, t0:t1],
                    r=[("h2", c, t0) for c in range(DC)], w=["hs"])
        if not final:
            for q in range(D // GQ):
                allgather(Hs[q * GQ:(q + 1) * GQ, :], Hg[q][:, :], r=["hs"], w=[("hg", q)])

    def load_H(H):
        for t, (t0, t1) in enumerate(TILES):
            pieces = []
            a = t0
            while a < t1:
                rk = a // HT
                b = min(t1, (rk + 1) * HT)
                pieces.append((a, b, rk))
                a = b
            for (a, b, rk) in pieces:
                for q in range(D // GQ):
                    dma("sp", H[:, 4 * q:4 * q + 4, a:b],
                        Hg[q][rk * GQ:(rk + 1) * GQ, a - rk * HT: b - rk * HT].rearrange("(c p) n -> p c n", p=128),
                        r=[("hg", q)], w=[("H", c, t) for c in range(4 * q, 4 * q + 4)])

    def store_Z(lc, src, c0, r):
        ids = []
        for h in range(2):
            a, b = max(c0, h * HT), (h + 1) * HT
            ids.append(dma("sp", Zs[h][lc * 128:(lc + 1) * 128, a - h * HT: b - h * HT], src[:, a:b], r=r, w=[("zs", h)]))
        return ids

    def gather_Z(nrows):
        for h in range(2):
            for q in range(nrows // GQ):
                allgather(Zs[h][q * GQ:(q + 1) * GQ, :], Zg[h][q][:, :], r=[("zs", h)], w=[("zg", h, q)])

    def lru_phase(i):
        j = i // 2
        S.fence()
        o_ = DYN
        H, o_ = alloc_at(o_, [128, DC, NTOK], BF16)
        UL = 2312
        U, o_ = alloc_at(o_, [128, 2, UL], F32)
        UCB, o_ = alloc_at(o_, [128, 2, NTOK], BF16)
        G, o_ = alloc_at(o_, [128, 2, NTOK], BF16)
        GT = []
        for _ in range(2):
            t, o_ = alloc_at(o_, [128, 512], F32)
            GT.append(t)
        Y = []
        for _ in range(2):
            t, o_ = alloc_at(o_, [128, NTOK], F32)
            Y.append(t)
        ZB1, o_ = alloc_at(o_, [128, NTOK], BF16)
        ZB = [ZB1, ZB1]
        NT = 2
        RB, IB, MB, HB_ = [], [], [], []
        for _ in range(NT):
            t, o_ = alloc_at(o_, [128, 512], F32); RB.append(t)
            t, o_ = alloc_at(o_, [128, 512], F32); IB.append(t)
            t, o_ = alloc_at(o_, [128, 512], F32); MB.append(t)
            t, o_ = alloc_at(o_, [128, 512], F32); HB_.append(t)
        load_H(H)
        dve(lambda e: e.memset(U[:, :, :], 0.0), w=[("U", 0), ("U", 1)])
        tctr = [0]
        for b in range(NBL):
            sg, su, sw = ring_next(), ring_next(), ring_next()
            wg = RING[sg][:, :].rearrange("p (k n) -> p k n", k=DC)
            wu = RING[su][:, :].rearrange("p (k n) -> p k n", k=DC)
            load_w(wg, lwin_g[j, :, b * 256:(b + 1) * 256].rearrange("(k p) n -> p k n", p=128), sg)
            load_w(wu, lwin_u[j, :, b * 256:(b + 1) * 256].rearrange("(k p) n -> p k n", p=128), su)
            wgt = RING[sw][:, 0:2048].rearrange("p (d g k n) -> p d g k n", d=2, g=2, k=2)
            for d in range(2):
                load_w(wgt[:, d, 0], lwa[j, d, b].rearrange("(k p) n -> p k n", p=128), sw)
                load_w(wgt[:, d, 1], lwx[j, d, b].rearrange("(k p) n -> p k n", p=128), sw)
            for cc in range(2):
                ch = b * 2 + cc
                for t in range(5):
                    t0, t1 = TILES[t]
                    w = t1 - t0
                    pb, pk = bank()
                    for k in range(DC):
                        mm(pb(w), wg[:, k, cc * 128:(cc + 1) * 128], H[:, k, t0:t1], k == 0, k == DC - 1,
                           r=[("ring", sg), ("H", k, t)], w=[pk])
                    q = tctr[0] % 2
                    tctr[0] += 1
                    gt = GT[q]
                    act(gt[:, :w], pb(w), AF.Square, r=[pk], w=[("gt", q)])
                    dve(lambda e, gt=gt, w=w: e.tensor_scalar(gt[:, :w], gt[:, :w], 0.044715, 1.0, ALU.mult, ALU.add),
                        r=[("gt", q)], w=[("gt", q)])
                    dve(lambda e, gt=gt, w=w, pb=pb: e.tensor_tensor(gt[:, :w], gt[:, :w], pb(w), ALU.mult),
                        r=[("gt", q), pk], w=[("gt", q)])
                    act(gt[:, :w], gt[:, :w], AF.Tanh, scale=0.7978845608028654, r=[("gt", q)], w=[("gt", q)])
                    dve(lambda e, gt=gt, w=w, pb=pb, cc=cc, t0=t0, t1=t1: e.scalar_tensor_tensor(G[:, cc, t0:t1], gt[:, :w], 1.0, pb(w), ALU.add, ALU.mult),
                        r=[("gt", q), pk], w=[("G", cc, t)])
                for t in range(5):
                    t0, t1 = TILES[t]
                    w = t1 - t0
                    pb, pk = bank()
                    for k in range(DC):
                        mm(pb(w), wu[:, k, cc * 128:(cc + 1) * 128], H[:, k, t0:t1], k == 0, k == DC - 1,
                           r=[("ring", su), ("H", k, t)], w=[pk])
                    uo = 2 + t0 if t == 0 else 261 + (t0 - CTX)
                    act(U[:, cc, uo:uo + w], pb(w), AF.Copy, r=[pk], w=[("U", cc)])
                for (s0, n, d0) in [(0, CTX, 0), (259, SEQ, CTX)]:
                    dve(lambda e, cc=cc, ch=ch, s0=s0, n=n, d0=d0: e.tensor_scalar(
                        Y[cc][:, d0:d0 + n], U[:, cc, s0:s0 + n], LCW[:, j, ch, 0:1], LCB[:, j, ch:ch + 1], ALU.mult, ALU.add),
                        r=[("U", cc), "lcw", "lcb"], w=[("y", cc)])
                    for tap in range(1, 4):
                        last = tap == 3
                        dve(lambda e, cc=cc, ch=ch, s0=s0, n=n, d0=d0, tap=tap, last=last: e.scalar_tensor_tensor(
                            (UCB[:, cc, d0:d0 + n] if last else Y[cc][:, d0:d0 + n]), U[:, cc, s0 + tap:s0 + tap + n],
                            LCW[:, j, ch, tap:tap + 1], Y[cc][:, d0:d0 + n], ALU.mult, ALU.add),
                            r=[("U", cc), "lcw", ("y", cc)], w=[("ucb", cc)] if last else [("y", cc)])
            for cc in range(2):
                ch = b * 2 + cc
                y = Y[cc]
                for d in range(2):
                    order = [0, 1, 2, 3, 4] if d == 0 else [0, 4, 3, 2, 1]
                    prev = None
                    pq = None
                    groups = [order[0:2], order[2:4], order[4:5]]
                    si = 0
                    for grp in groups:
                        items = []
                        for t in grp:
                            q = tctr[0] % NT
                            tctr[0] += 1
                            items.append((t, q, si))
                            si += 1
                        for (t, q, _) in items:
                            t0, t1 = TILES[t]
                            w = t1 - t0
                            R_, I_, M_ = RB[q], IB[q], MB[q]
                            for gi, dst in enumerate([R_, I_]):
                                pb, pk = bank()
                                for k in range(2):
                                    mm(pb(w), wgt[:, d, gi, k, cc * 128:(cc + 1) * 128], UCB[:, k, t0:t1], k == 0, k == 1,
                                       r=[("ring", sw), ("ucb", 0), ("ucb", 1)], w=[pk])
                                bsrc = LBA if gi == 0 else LBX
                                act(dst[:, :w], pb(w), AF.Tanh, bias=bsrc[:, j, d, ch:ch + 1], scale=0.5, r=[pk, "lba", "lbx"], w=[("rim", q, gi)])
                            act(M_[:, :w], R_[:, :w], AF.Exp, bias=LSA[:, j, d, ch:ch + 1], scale=LSA[:, j, d, ch:ch + 1],
                                r=[("rim", q, 0), "lsa"], w=[("rim", q, 2)])
                            act(R_[:, :w], R_[:, :w], AF.Exp, bias=LSAH[:, j, d, ch:ch + 1], scale=LSAH[:, j, d, ch:ch + 1],
                                r=[("rim", q, 0), ("rim", q, 2), "lsah"], w=[("rim", q, 0)])
                        for (t, q, _) in items:
                            w = TILES[t][1] - TILES[t][0]
                            M_ = MB[q]
                            act(M_[:, :w], M_[:, :w], AF.Sqrt, bias=0.25, scale=-0.25, r=[("rim", q, 2)], w=[("rim", q, 2)])
                        for (t, q, si_) in items:
                            t0, t1 = TILES[t]
                            w = t1 - t0
                            R_, I_, M_, Hs_ = RB[q], IB[q], MB[q], HB_[q]
                            dve(lambda e, I_=I_, w=w, cc=cc, t0=t0, t1=t1: e.scalar_tensor_tensor(I_[:, :w], I_[:, :w], 1.0, UCB[:, cc, t0:t1], ALU.add, ALU.mult),
                                r=[("rim", q, 1), ("ucb", cc)], w=[("rim", q, 1)])
                            dve(lambda e, I_=I_, M_=M_, w=w: e.tensor_tensor(I_[:, :w], I_[:, :w], M_[:, :w], ALU.mult),
                                r=[("rim", q, 1), ("rim", q, 2)], w=[("rim", q, 1)])
                            if d == 0:
                                init = 0.0 if si_ == 0 else y[:, t0 - 1:t0]
                                dve(lambda e, y=y, R_=R_, I_=I_, w=w, t0=t0, t1=t1, init=init: e.tensor_tensor_scan(
                                    y[:, t0:t1], R_[:, :w], I_[:, :w], init, ALU.mult, ALU.add),
                                    r=[("rim", q, 0), ("rim", q, 1), ("y", cc)], w=[("y", cc)])
                            else:
                                init = 0.0 if si_ == 0 else prev[:, 0:1]
                                dve(lambda e, Hs_=Hs_, R_=R_, I_=I_, w=w, init=init: e.tensor_tensor_scan(
                                    Hs_[:, 0:w][:, ::-1], R_[:, 0:w][:, ::-1], I_[:, 0:w][:, ::-1], init, ALU.mult, ALU.add),
                                    r=[("rim", q, 0), ("rim", q, 1)] + ([("rim", pq, 3)] if prev is not None else []), w=[("rim", q, 3)])
                                dve(lambda e, y=y, Hs_=Hs_, w=w, t0=t0, t1=t1: e.tensor_tensor(y[:, t0:t1], y[:, t0:t1], Hs_[:, :w], ALU.add),
                                    r=[("rim", q, 3), ("y", cc)], w=[("y", cc)])
                                prev = Hs_
                                pq = q
                dve(lambda e, y=y, cc=cc: e.scalar_tensor_tensor(ZB[cc][:, :], y[:, :], 0.5, G[:, cc, :], ALU.mult, ALU.mult),
                    r=[("y", cc)] + [("G", cc, t) for t in range(5)], w=["zb"])
                store_Z(ch, ZB[cc], 0, ["zb"])
            drain(9)
        drain(1000)
        gather_Z(NBL * 256)

    def na_phase(i):
        j = i // 2
        need_ctx = i < DEPTH - 1
        S.fence()
        o_ = DYN
        H, o_ = alloc_at(o_, [128, DC, NTOK], BF16)
        QTB, KTB, VB = [], [], []
        for _ in range(2):
            t, o_ = alloc_at(o_, [128, NTOK], BF16); QTB.append(t)
            t, o_ = alloc_at(o_, [128, NTOK], BF16); KTB.append(t)
            t, o_ = alloc_at(o_, [128, 18, 128], BF16); VB.append(t)
        OT = []
        for _ in range(2):
            t, o_ = alloc_at(o_, [128, NTOK], BF16)
            OT.append(t)
        BI1, o_ = alloc_at(o_, [128, 5 * NKEY], F32)
        SS_, PP, PT = [], [], []
        for _ in range(2):
            t, o_ = alloc_at(o_, [128, NKEY], F32); SS_.append(t)
            t, o_ = alloc_at(o_, [128, NKEY], BF16); PP.append(t)
            t, o_ = alloc_at(o_, [128, 7, 128], BF16); PT.append(t)
        ST, o_ = alloc_at(o_, [128, 2, 4], F32)
        load_H(H)
        qctr = [0]
        sc = 128 ** -0.5

        def make_proj(hd):
            hb = hd % 2
            QT_, KT_, V_ = QTB[hb], KTB[hb], VB[hb]
            st8 = {}

            def s_load():
                s1, s2 = ring_next(), ring_next()
                st8["s1"], st8["s2"] = s1, s2
                st8["wq"] = RING[s1][:, 0:2048].rearrange("p (k n) -> p k n", k=DC)
                st8["wk"] = RING[s1][:, 2048:4096].rearrange("p (k n) -> p k n", k=DC)
                st8["wv"] = RING[s2][:, 0:2048].rearrange("p (k n) -> p k n", k=DC)
                load_w(st8["wq"], nwq[j, :, hd * 128:(hd + 1) * 128].rearrange("(k p) n -> p k n", p=128), s1)
                load_w(st8["wk"], nwk[j, :, hd * 128:(hd + 1) * 128].rearrange("(k p) n -> p k n", p=128), s1)
                load_w(st8["wv"], nwv[j, :, hd * 128:(hd + 1) * 128].rearrange("(k p) n -> p k n", p=128), s2)

            def s_q(t):
                t0, t1 = TILES[t]
                w = t1 - t0
                pb, pk = bank(3)
                for k in range(DC):
                    mm(pb(w), st8["wq"][:, k, :], H[:, k, t0:t1], k == 0, k == DC - 1, r=[("ring", st8["s1"]), ("H", k, t)], w=[pk])
                act(QT_[:, t0:t1], pb(w), AF.Copy, scale=sc, r=[pk], w=[("qt", hb, t)])

            def s_k(t):
                t0, t1 = TILES[t]
                w = t1 - t0
                pb, pk = bank(3)
                for k in range(DC):
                    mm(pb(w), st8["wk"][:, k, :], H[:, k, t0:t1], k == 0, k == DC - 1, r=[("ring", st8["s1"]), ("H", k, t)], w=[pk])
                dve(lambda e: e.tensor_copy(KT_[:, t0:t1], pb(w)), r=[pk], w=[("kt", hb, t)])

            def s_v(tc4):
                n = min(4, 18 - tc4)
                pb, pk = bank(3)
                for u in range(n):
                    tc = tc4 + u
                    tt = [x for x in range(5) if TILES[x][0] <= tc * 128 < TILES[x][1]][0]
                    for k in range(DC):
                        mm(pb(512)[:, u * 128:(u + 1) * 128], H[:, k, tc * 128:(tc + 1) * 128], st8["wv"][:, k, :], k == 0, k == DC - 1,
                           r=[("ring", st8["s2"]), ("H", k, tt)], w=[pk])
                act(V_[:, tc4:tc4 + n, :], pb(n * 128).rearrange("p (a b) -> p a b", a=n), AF.Copy, r=[pk],
                    w=[("v", hb, x) for x in range(tc4, tc4 + n)])

            steps = [s_load]
            for t in range(5):
                if t > 0 or need_ctx:
                    steps.append(lambda t=t: s_q(t))
                steps.append(lambda t=t: s_k(t))
            for tc4 in range(0, 18, 4):
                steps.append(lambda tc4=tc4: s_v(tc4))
            return steps

        nxt = make_proj(0)
        for st_ in nxt:
            st_()
        for hd in range(NHL):
            hb = hd % 2
            QT, KT, V = QTB[hb], KTB[hb], VB[hb]
            bi = BI1
            dma("sp", bi[:, :], nbias[j, hd], w=["bi"])
            ot = OT[hd % 2]
            nxt = make_proj(hd + 1) if hd + 1 < NHL else []
            qtiles = ([("c", 0), ("c", 1)] if need_ctx else []) + [("l", x) for x in range(16)]
            infos = []
            for kind, jj in qtiles:
                q = qctr[0] % 2
                qctr[0] += 1
                inf = dict(kind=kind, q=q, sps=(sA if q == 0 else sB),
                           spk=([("ps", 4), ("ps", 5)] if q == 0 else [("ps", 6), ("ps", 7)]))
                if kind == "c":
                    inf.update(q0=jj * 128, nk=CTX, segs=[(0, 0, CTX)], qtk=("qt", hb, 0), ktk=[("kt", hb, 0)], k0=0, pat=0)
                else:
                    base = min(max(2 * jj - 4, 0), 22)
                    k0 = CTX + base * 64
                    inf.update(q0=CTX + jj * 128, nk=NKEY, segs=[(0, 0, 256), (256, k0, 256), (512, k0 + 256, 384)],
                               qtk=("qt", hb, 1 + jj // 4), ktk=[("kt", hb, x) for x in range(5)], k0=k0,
                               pat=(0 if jj == 0 else 1 if jj == 1 else 2 if jj <= 13 else 3 if jj == 14 else 4))
                infos.append(inf)

            def stage_A(f):
                for (so, ko, kw_) in f["segs"]:
                    mm(f["sps"][:, so:so + kw_], QT[:, f["q0"]:f["q0"] + 128], KT[:, ko:ko + kw_], True, True,
                       r=[f["qtk"]] + f["ktk"], w=f["spk"])

            def stage_B(f):
                q, nk, sps = f["q"], f["nk"], f["sps"]
                ss, pp = SS_[q], PP[q]
                bi_ = bi
                if f["kind"] == "c":
                    dve(lambda e: e.tensor_copy(ss[:, :nk], sps[:, :nk]), r=f["spk"], w=[("ss", q)])
                else:
                    pat = f["pat"]
                    dve(lambda e: e.tensor_tensor(ss[:, :], sps[:, 0:NKEY], bi_[:, pat * NKEY:(pat + 1) * NKEY], ALU.add),
                        r=f["spk"] + ["bi"], w=[("ss", q)])
                dve(lambda e: e.tensor_reduce(ST[:, q, 0:1], ss[:, :nk], AX.X, ALU.max), r=[("ss", q)], w=[("st", q, 0)])
                dve(lambda e: e.tensor_scalar(ST[:, q, 1:2], ST[:, q, 0:1], -1.0, None, ALU.mult), r=[("st", q, 0)], w=[("st", q, 1)])
                act(ss[:, :nk], ss[:, :nk], AF.Exp, bias=ST[:, q, 1:2], r=[("ss", q), ("st", q, 1)], w=[("ss", q), ("st", q, 2)],
                    accum=ST[:, q, 2:3])
                dve(lambda e: e.reciprocal(ST[:, q, 3:4], ST[:, q, 2:3]), r=[("st", q, 2)], w=[("st", q, 3)])
                act(pp[:, :nk], ss[:, :nk], AF.Identity, scale=ST[:, q, 3:4], r=[("ss", q), ("st", q, 3)], w=[("pp", q)])

            def stage_C(f):
                q, nk = f["q"], f["nk"]
                pp, pt = PP[q], PT[q]
                nkc = nk // 128
                for kc in range(nkc):
                    S.add("pe", lambda e, kc=kc: e.transpose(ptb[:, kc * 128:(kc + 1) * 128], pp[:, kc * 128:(kc + 1) * 128], ident[:]),
                          r=[("pp", q), "ident"], w=[("ps", 3)])
                act(pt[:, :nkc, :], ptb[:, :nkc * 128].rearrange("p (a b) -> p a b", a=nkc), AF.Copy, r=[("ps", 3)], w=[("pt", q)])
                pb, pk = bank(3)
                for kc in range(nkc):
                    vc = kc if (f["kind"] == "c" or kc < 2) else f["k0"] // 128 + (kc - 2)
                    mm(pb(128), V[:, vc, :], pt[:, kc, :], kc == 0, kc == nkc - 1, r=[("v", hb, vc), ("pt", q)], w=[pk])
                q0 = f["q0"]
                ot_ = ot
                act(ot_[:, q0:q0 + 128], pb(128), AF.Identity, r=[pk], w=[("ot", hd % 2)])

            n_t = len(infos)
            if PIPE_NA:
                for step in range(n_t + 2):
                    if step < n_t:
                        stage_A(infos[step])
                    if 0 <= step - 1 < n_t:
                        stage_B(infos[step - 1])
                    if 0 <= step - 2 < n_t:
                        stage_C(infos[step - 2])
                    if nxt:
                        nxt.pop(0)()
                while nxt:
                    nxt.pop(0)()
            else:
                for f_ in infos:
                    stage_A(f_)
                    stage_B(f_)
                    stage_C(f_)
            store_Z(hd, ot, 0 if need_ctx else CTX, [("ot", hd % 2)])
            drain(7)
        drain(1000)
        gather_Z(NHL * 128)

    mods_phase(0)
    token_phase(-1, False)
    for i in range(n_layers):
        last = (i == n_layers - 1)
        if not last and OVERLAP_MODS:
            PENDING.extend(mods_steps(i + 1))
        if i % 2 == 0:
            lru_phase(i)
        else:
            na_phase(i)
        if not last and not OVERLAP_MODS:
            mods_phase(i + 1)
        token_phase(i, last)
    S.wait_all("sp", OUT_IDS)
    S.emit(nc)
    return nc, S


def _fm(v):
    v = np.asarray(v, np.float32)
    lead = v.shape[:-1]
    n = v.shape[-1] // 128
    v = v.reshape(lead + (n, 128))
    return np.ascontiguousarray(np.moveaxis(v, -1, 0))


def _na_bias(rpb):
    L = rpb.shape[0]
    out = np.zeros((L, NH, 128, 5, NKEY), np.float32)
    reps = [0, 1, 2, 14, 15]
    qi = np.arange(128)
    kk = np.arange(640)
    for p, jj in enumerate(reps):
        base = min(max(2 * jj - 4, 0), 22)
        qr = 2 * jj + qi // 64
        qc = qi % 64
        kr = base + kk // 64
        kc = kk % 64
        rs = np.clip(qr - 4, 0, 24)
        cs = np.clip(qc - 8, 0, 48)
        valid = ((kr[None, :] >= rs[:, None]) & (kr[None, :] < rs[:, None] + 8)
                 & (kc[None, :] >= cs[:, None]) & (kc[None, :] < cs[:, None] + 16))
        di = np.clip(kr[None, :] - qr[:, None] + 7, 0, 14)
        dj = np.clip(kc[None, :] - qc[:, None] + 15, 0, 30)
        g = rpb[:, :, di, dj]
        out[:, :, :, p, 256:] = np.where(valid[None, None], g, np.float32(NEG))
    return out.reshape(L, NH, 128, 5 * NKEY)


def _pad_last(a, n):
    if a.shape[-1] == n:
        return np.ascontiguousarray(a)
    out = np.zeros(a.shape[:-1] + (n,), a.dtype)
    out[..., :a.shape[-1]] = a
    return out


def _prep_inputs(inp):
    f = lambda k: np.ascontiguousarray(np.asarray(inp[k], np.float32))
    x, c, ctx, c_ctx = f("x"), f("c"), f("ctx"), f("c_ctx")
    WP = 2 * NBL * 256
    w_in = f("lru_w_in")
    w_g = _pad_last(w_in[:, :, :W_LRU], WP)
    w_u = _pad_last(w_in[:, :, W_LRU:], WP)
    cw = _pad_last(f("lru_conv_w"), WP)
    cb = _pad_last(f("lru_conv_b"), WP)
    lam = _pad_last(f("lru_lambda"), WP)
    ba = _pad_last(f("lru_ba").reshape(2, 2, W_LRU), WP)
    bx = _pad_last(f("lru_bx").reshape(2, 2, W_LRU), WP)
    wa = np.zeros((2, 2, 2 * NBL, 256, 256), np.float32); wa[:, :, :NBLK] = f("lru_wa")
    wx = np.zeros((2, 2, 2 * NBL, 256, 256), np.float32); wx[:, :, :NBLK] = f("lru_wx")
    w_out = np.zeros((2, WP, D), np.float32); w_out[:, :W_LRU] = f("lru_w_out")
    qkv = f("na_w_qkv")
    bias = _na_bias(f("na_rpb"))
    shared = {
        "ident": np.eye(128, dtype=np.float32),
        "ada_w": f("ada_w"),
        "ada_b": np.ascontiguousarray(_fm(f("ada_b")).reshape(128, DEPTH, 96)),
        "n1g": _fm(f("norm1_g")), "n2g": _fm(f("norm2_g")), "fing": _fm(f("final_g")),
        "mlp_w1": f("mlp_w1"), "mlp_w2": f("mlp_w2"),
        "lru_w_out": w_out, "na_w_o": f("na_w_o"),
    }
    per_rank = []
    for rk in range(2):
        cs = slice(rk * NBL * 256, (rk + 1) * NBL * 256)
        hs = slice(rk * NHL * 128, (rk + 1) * NHL * 128)
        per_rank.append({
            "lru_w_g": np.ascontiguousarray(w_g[:, :, cs]), "lru_w_u": np.ascontiguousarray(w_u[:, :, cs]),
            "lru_cw": np.ascontiguousarray(np.transpose(_fm(cw[:, :, cs]), (0, 1, 3, 2))),
            "lru_cb": _fm(cb[:, cs]), "lru_lam": _fm(lam[:, :, cs]),
            "lru_ba": _fm(ba[:, :, cs]), "lru_bx": _fm(bx[:, :, cs]),
            "lru_wa": np.ascontiguousarray(wa[:, :, rk * NBL:(rk + 1) * NBL]),
            "lru_wx": np.ascontiguousarray(wx[:, :, rk * NBL:(rk + 1) * NBL]),
            "na_w_q": np.ascontiguousarray(qkv[:, :, 0:D][:, :, hs]),
            "na_w_k": np.ascontiguousarray(qkv[:, :, D:2 * D][:, :, hs]),
            "na_w_v": np.ascontiguousarray(qkv[:, :, 2 * D:3 * D][:, :, hs]),
            "na_bias": np.ascontiguousarray(bias[:, rk * NHL:(rk + 1) * NHL]),
            "flags": np.ascontiguousarray(np.tile(np.array([[1.0 - rk, float(rk)]], np.float32), (128, 1))),
        })
    maps = []
    for core in range(8):
        b, rk = core // 2, core % 2
        m = dict(shared)
        m.update(per_rank[rk])
        full = np.concatenate([ctx[b], x[b]], axis=0)
        m["xT"] = np.ascontiguousarray(full[rk * HT:(rk + 1) * HT].T)
        m["cv"] = np.ascontiguousarray(np.stack([_fm(c[b]), _fm(c_ctx)], axis=-1))
        maps.append(m)
    return maps


_CACHE = {}


def kernel(**inputs):
    maps = _prep_inputs(inputs)
    if "nc" not in _CACHE:
        _CACHE["nc"] = build_program()[0]
    res = run_bass_kernel_spmd(_CACHE["nc"], maps, core_ids=list(range(8)))
    out = np.empty((4, SEQ, D), np.float32)
    for b in range(4):
        o0 = res.results[2 * b]["outT"]
        o1 = res.results[2 * b + 1]["outT"]
        out[b, :HT - CTX] = o0[:, CTX:].T
        out[b, HT - CTX:] = o1.T
    return out
```

```python
import numpy as np
import concourse.bass as bass
import concourse.mybir as mybir
from concourse.bass_utils import run_bass_kernel_spmd

F32 = mybir.dt.float32
BF16 = mybir.dt.bfloat16
AF = mybir.ActivationFunctionType
ALU = mybir.AluOpType
AX = mybir.AxisListType

D = 2048
DC = 16
NTOK = 2304
CTX = 256
SEQ = 2048
DEPTH = 4
W_LRU = 2816
LC = 22
NBLK = 11
DFF = 8192
NH = 16
EPS = 1e-6
TILES = [(0, 256), (256, 768), (768, 1280), (1280, 1792), (1792, 2304)]
HT = 1152
LT = [(0, 256), (256, 768), (768, 1152)]
NBL = 6
NHL = 8
PAIRS = [[0, 1], [2, 3], [4, 5], [6, 7]]
NEG = -30000.0
NKEY = 896


class Sched:
    ENG = ["pe", "act", "dve", "pool", "sp"]

    def __init__(self):
        self.ops = []
        self.by_eng = {e: [] for e in self.ENG}
        self.wr = {}
        self.rd = {}
        self.fence_ops = []
        self.dma_count = {"pool": 0, "sp": 0, "act": 0}
        self.dma_last = {}
        self.ndma = {"pool": 20, "sp": 20, "act": 4}
        self.last_cc = None

    def add(self, eng, fn, r=(), w=(), kind="c"):
        i = len(self.ops)
        op = dict(id=i, eng=eng, fn=fn, kind=kind, deps={}, users=False)
        rkey = eng if kind == "c" else ("dma", i)
        for k in r:
            if k in self.wr:
                op["deps"][self.wr[k]] = "raw"
            self.rd.setdefault(k, {})[rkey] = i
        for k in w:
            if k in self.wr:
                op["deps"].setdefault(self.wr[k], "waw")
            for j in self.rd.get(k, {}).values():
                if j != i:
                    op["deps"].setdefault(j, "war")
            self.wr[k] = i
            self.rd[k] = {}
        for j in self.fence_ops:
            op["deps"].setdefault(j, "raw")
        if kind == "dma":
            n = self.dma_count[eng]
            self.dma_count[eng] = n + 1
            slot = (eng, n % self.ndma[eng])
            op["dslot"] = slot
            op["dval"] = 16 * (n // self.ndma[eng] + 1)
            if slot in self.dma_last:
                op["deps"].setdefault(self.dma_last[slot], "raw")
            self.dma_last[slot] = i
        if kind == "cc":
            self.last_cc = i
        self.ops.append(op)
        self.by_eng[eng].append(i)
        return i

    def fence(self):
        last = []
        for e in self.ENG:
            if self.by_eng[e]:
                last.append(self.by_eng[e][-1])
        last += list(self.dma_last.values())
        if self.last_cc is not None:
            last.append(self.last_cc)
        self.fence_ops = sorted(set(last))

    def wait_all(self, eng, ids):
        i = self.add(eng, None)
        for j in ids:
            self.ops[i]["deps"][j] = "raw"
        return i

    def emit(self, nc):
        ops = self.ops
        for op in ops:
            nd = {}
            for j, t in op["deps"].items():
                p = ops[j]
                if p["kind"] == "c" and p["eng"] == op["eng"]:
                    if op["eng"] == "pe" or t != "raw":
                        continue
                nd[j] = t
            op["deps"] = nd
            for j in nd:
                ops[j]["users"] = True
        cnt = {e: 0 for e in self.ENG}
        ncc = 0
        for op in ops:
            if op["kind"] == "c" and op["users"] and op["fn"] is not None:
                cnt[op["eng"]] += 1
                op["cval"] = cnt[op["eng"]]
            if op["kind"] == "cc":
                ncc += 1
                op["ccval"] = ncc
        import contextlib
        st = contextlib.ExitStack()
        sem_e = {e: st.enter_context(nc.semaphore("prog_" + e)) for e in self.ENG}
        sem_cc = st.enter_context(nc.semaphore("prog_cc"))
        sem_d = {}
        for q, n in self.ndma.items():
            for k in range(n):
                sem_d[(q, k)] = st.enter_context(nc.semaphore("dma_%s_%d" % (q, k)))
        plans = {}
        for e in self.ENG:
            seen_c = {f: 0 for f in self.ENG}
            seen_d = {}
            seen_cc = 0
            plan = []
            for i in self.by_eng[e]:
                op = ops[i]
                need_c = {}
                need_d = {}
                need_cc = 0
                for j in op["deps"]:
                    p = ops[j]
                    if p["kind"] == "cc":
                        need_cc = max(need_cc, p["ccval"])
                    elif p["kind"] == "c":
                        if p["fn"] is None:
                            continue
                        need_c[p["eng"]] = max(need_c.get(p["eng"], 0), p["cval"])
                    else:
                        need_d[p["dslot"]] = max(need_d.get(p["dslot"], 0), p["dval"])
                waits = []
                if need_cc > seen_cc:
                    seen_cc = need_cc
                    waits.append((sem_cc, need_cc))
                for f, v in need_c.items():
                    if v > seen_c[f]:
                        seen_c[f] = v
                        waits.append((sem_e[f], v))
                for sl, v in need_d.items():
                    if v > seen_d.get(sl, 0):
                        seen_d[sl] = v
                        waits.append((sem_d[sl], v))
                plan.append((op, waits))
            plans[e] = plan
        self.plans = plans
        self.sem_names = {id(v): k for k, v in list(sem_e.items()) + list(sem_d.items()) + [("cc", sem_cc)]}
        self.stats = {e: len(plans[e]) for e in self.ENG}
        self.stats["incs"] = dict(cnt)

        def run(e, eng):
            for op, waits in plans[e]:
                for s, v in waits:
                    eng.wait_ge(s, v)
                if op["fn"] is None:
                    continue
                ins = op["fn"](eng)
                if op["kind"] == "dma":
                    ins.then_inc(sem_d[op["dslot"]], 16)
                elif op["kind"] == "cc":
                    ins.then_inc(sem_cc, 1)
                elif op["users"]:
                    ins.then_inc(sem_e[e], 1)

        with st:
            with nc.Block() as block:
                @block.tensor
                def _(eng):
                    run("pe", eng)

                @block.scalar
                def _(eng):
                    run("act", eng)

                @block.vector
                def _(eng):
                    run("dve", eng)

                @block.gpsimd
                def _(eng):
                    run("pool", eng)

                @block.sync
                def _(eng):
                    run("sp", eng)


PIPE_NA = True
OVERLAP_MODS = True
CC_SEM = False


def build_program(n_layers=DEPTH, debug=False):
    nc = bass.Bass("TRN2", target_bir_lowering=False)
    S = Sched()

    def din(name, shape, dt=F32):
        return nc.dram_tensor(name, list(shape), dt, kind="ExternalInput").ap()

    xT = din("xT", [D, HT])
    cv = din("cv", [128, DC, 2])
    flg = din("flags", [128, 2])
    ident_d = din("ident", [128, 128])
    ada_w = din("ada_w", [DEPTH, D, 6 * D])
    ada_b = din("ada_b", [128, DEPTH, 96])
    n1g = din("n1g", [128, DEPTH, DC])
    n2g = din("n2g", [128, DEPTH, DC])
    fing = din("fing", [128, DC])
    w1d = din("mlp_w1", [DEPTH, D, DFF])
    w2d = din("mlp_w2", [DEPTH, DFF, D])
    LCL = 2 * NBL
    lwin_g = din("lru_w_g", [2, D, NBL * 256])
    lwin_u = din("lru_w_u", [2, D, NBL * 256])
    lcw = din("lru_cw", [128, 2, LCL, 4])
    lcb = din("lru_cb", [128, 2, LCL])
    llam = din("lru_lam", [128, 2, 2, LCL])
    lwa = din("lru_wa", [2, 2, NBL, 256, 256])
    lba = din("lru_ba", [128, 2, 2, LCL])
    lwx = din("lru_wx", [2, 2, NBL, 256, 256])
    lbx = din("lru_bx", [128, 2, 2, LCL])
    lwout = din("lru_w_out", [2, 2 * NBL * 256, D])
    nwq = din("na_w_q", [2, D, NHL * 128])
    nwk = din("na_w_k", [2, D, NHL * 128])
    nwv = din("na_w_v", [2, D, NHL * 128])
    nwo = din("na_w_o", [2, D, D])
    nbias = din("na_bias", [2, NHL, 128, 5 * NKEY])
    outT = nc.dram_tensor("outT", [D, HT], F32, kind="ExternalOutput").ap()
    kw = dict(kind="ExternalOutput") if debug else {}
    CZ = NBL * 256
    Xm = nc.dram_tensor("Xm", [D, HT], F32, **kw).ap()
    Hs = nc.dram_tensor("Hs", [D, HT], BF16).ap()
    GQ = 512
    Hg = [nc.dram_tensor("Hg%d" % q, [2 * GQ, HT], BF16, **kw).ap() for q in range(D // GQ)]
    Zs = [nc.dram_tensor("Zs%d" % h, [CZ, HT], BF16).ap() for h in range(2)]
    Zg = [[nc.dram_tensor("Zg%d_%d" % (h, q), [2 * GQ, HT], BF16, **kw).ap() for q in range(CZ // GQ)] for h in range(2)]

    SB_TOTAL = 224 * 1024
    SB_BASE = ((SB_TOTAL - nc.sbuf_bytes_remaining + 63) // 64) * 64
    cap = SB_TOTAL - 2048
    names = [0]

    def esz(dt):
        return 4 if dt == F32 else 2

    def alloc_at(off, shape, dt):
        names[0] += 1
        n = 1
        for s_ in shape[1:]:
            n *= s_
        nbytes = n * esz(dt)
        assert off + nbytes <= cap, (off, nbytes, cap)
        t = nc.alloc_sbuf_tensor_at("t%d" % names[0], list(shape), dt, offset=off)
        return t, off + ((nbytes + 63) // 64) * 64

    off = SB_BASE
    ones, off = alloc_at(off, [128, 128], BF16)
    ident, off = alloc_at(off, [128, 128], BF16)
    SV, off = alloc_at(off, [128, DC, 2], BF16)
    CVs, off = alloc_at(off, [128, DC, 2], F32)
    FLG, off = alloc_at(off, [128, 2], F32)
    MODSB, MAB = [], []
    for _ in range(2):
        t_, off = alloc_at(off, [128, 96, 3], F32); MODSB.append(t_)
        t_, off = alloc_at(off, [128, 2, DC, 3], F32); MAB.append(t_)
    ADAB, off = alloc_at(off, [128, DEPTH, 96], F32)
    N1G, off = alloc_at(off, [128, DEPTH, DC], F32)
    N2G, off = alloc_at(off, [128, DEPTH, DC], F32)
    FING, off = alloc_at(off, [128, DC], F32)
    LCW, off = alloc_at(off, [128, 2, LCL, 4], F32)
    LCB, off = alloc_at(off, [128, 2, LCL], F32)
    LLAM, off = alloc_at(off, [128, 2, 2, LCL], F32)
    LSA, off = alloc_at(off, [128, 2, 2, LCL], F32)
    LSAH, off = alloc_at(off, [128, 2, 2, LCL], F32)
    LBA, off = alloc_at(off, [128, 2, 2, LCL], F32)
    LBX, off = alloc_at(off, [128, 2, 2, LCL], F32)
    NRING = 6
    RING = []
    for _ in range(NRING):
        t, off = alloc_at(off, [128, 4096], BF16)
        RING.append(t)
    DYN = off
    ring_ctr = [0]

    def ring_next():
        k = ring_ctr[0] % NRING
        ring_ctr[0] += 1
        return k

    pa = nc.alloc_psum_tensor("pa", [128, 1536], F32)
    ptb = nc.alloc_psum_tensor("ptb", [128, 1024], BF16)
    sA = nc.alloc_psum_tensor("sA", [128, 1024], F32)
    sB = nc.alloc_psum_tensor("sB", [128, 1024], F32)
    GB = [(pa, 0, 0), (pa, 512, 1), (pa, 1024, 2), (sA, 0, 4), (sA, 512, 5), (sB, 0, 6), (sB, 512, 7)]
    gb_ctr = [0]

    def bank(n=None):
        lst = GB if n is None else GB[:n]
        t, o, k = lst[gb_ctr[0] % len(lst)]
        gb_ctr[0] += 1
        return (lambda w, t=t, o=o: t[:, o:o + w]), ("ps", k)

    OUT_IDS = []

    def dma(q, out, in_, r=(), w=()):
        return S.add(q, lambda e: e.dma_start(out=out, in_=in_), r=r, w=w, kind="dma")

    def mm(out, lhsT, rhs, start, stop, r=(), w=()):
        return S.add("pe", lambda e: e.matmul(out, lhsT, rhs, start=start, stop=stop), r=r, w=w)

    def act(out, in_, func, bias=0.0, scale=1.0, r=(), w=(), accum=None):
        if accum is None:
            return S.add("act", lambda e: e.activation(out, in_, func, bias=bias, scale=scale), r=r, w=w)
        return S.add("act", lambda e: e.activation(out, in_, func, bias=bias, scale=scale, accum_out=accum), r=r, w=w)

    def dve(fn, r=(), w=()):
        return S.add("dve", fn, r=r, w=w)

    def load_w(dst_ap, src_ap, slot, extra_w=()):
        return dma("pool", dst_ap, src_ap, w=[("ring", slot)] + list(extra_w))

    def allgather(src, dst, r, w):
        return S.add("pool", lambda e: e.collective_compute("AllGather", ALU.bypass, ins=[src], outs=[dst], replica_groups=PAIRS),
                     r=r, w=w, kind=("cc" if CC_SEM else "c"))

    dve(lambda e: e.memset(ones[:], 1.0), w=["ones"])
    dma("pool", ident[:], ident_d, w=["ident"])
    dma("sp", CVs[:], cv, w=["cvs"])
    dma("sp", FLG[:], flg, w=["flg"])
    dma("sp", ADAB[:], ada_b, w=["adab"])
    dma("sp", N1G[:], n1g, w=["n1g"])
    dma("sp", N2G[:], n2g, w=["n2g"])
    dma("sp", FING[:], fing, w=["fing"])
    dma("sp", LCW[:], lcw, w=["lcw"])
    dma("sp", LCB[:], lcb, w=["lcb"])
    dma("sp", LLAM[:], llam, w=["llam"])
    dma("sp", LBA[:], lba, w=["lba"])
    dma("sp", LBX[:], lbx, w=["lbx"])
    act(SV[:], CVs[:], AF.Silu, r=["cvs"], w=["sv"])
    dve(lambda e: e.tensor_scalar(LBA[:], LBA[:], 0.5, None, ALU.mult), r=["lba"], w=["lba"])
    dve(lambda e: e.tensor_scalar(LBX[:], LBX[:], 0.5, None, ALU.mult), r=["lbx"], w=["lbx"])
    act(LSA[:], LLAM[:], AF.Exp, scale=-1.0, r=["llam"], w=["lsa"])
    act(LSA[:], LSA[:], AF.Ln, bias=1.0, r=["lsa"], w=["lsa"])
    dve(lambda e: e.tensor_scalar(LSAH[:], LSA[:], -4.0, None, ALU.mult), r=["lsa"], w=["lsah"])
    dve(lambda e: e.tensor_scalar(LSA[:], LSA[:], -8.0, None, ALU.mult), r=["lsa", "lsah"], w=["lsa"])


    def mods_steps(i):
        MODS, MA = MODSB[i % 2], MAB[i % 2]
        mk, ak = ("mods", i % 2), ("ma", i % 2)

        def piece(pc):
            slot = ring_next()
            wt = RING[slot][:, 0:4096].rearrange("p (k n) -> p k n", k=DC)
            load_w(wt, ada_w[i, :, pc * 256:(pc + 1) * 256].rearrange("(k p) n -> p k n", p=128), slot)
            for f in range(2):
                fc = pc * 2 + f
                pb, pk = bank(3)
                for k in range(DC):
                    mm(pb(2), wt[:, k, f * 128:(f + 1) * 128], SV[:, k, :], k == 0, k == DC - 1,
                       r=[("ring", slot), "sv"], w=[pk])
                dve(lambda e, pb=pb, fc=fc: e.tensor_scalar(MODS[:, fc, 0:2], pb(2), ADAB[:, i, fc:fc + 1], None, ALU.add),
                    r=[pk, "adab"], w=[mk])

        def derived():
            dve(lambda e: e.tensor_scalar(MODS[:, :, 2], MODS[:, :, 1], FLG[:, 0:1], None, ALU.mult), r=[mk, "flg"], w=[mk])
            dve(lambda e: e.scalar_tensor_tensor(MODS[:, :, 2], MODS[:, :, 0], FLG[:, 1:2], MODS[:, :, 2], ALU.mult, ALU.add),
                r=[mk, "flg"], w=[mk])
            for n, (gsrc, base) in enumerate([(N1G, 16), (N2G, 64)]):
                for s_ in range(3):
                    dve(lambda e, n=n, s_=s_, base=base: e.tensor_scalar(MA[:, n, :, s_], MODS[:, base:base + 16, s_], 1.0, None, ALU.add),
                        r=[mk], w=[ak])
                    dve(lambda e, n=n, s_=s_, gsrc=gsrc: e.tensor_tensor(MA[:, n, :, s_], MA[:, n, :, s_], gsrc[:, i, :], ALU.mult),
                        r=[ak, "n1g", "n2g"], w=[ak])
        return [(lambda pc=pc: piece(pc)) for pc in range(48)] + [derived]

    def mods_phase(i):
        S.fence()
        for st_ in mods_steps(i):
            st_()

    PENDING = []

    def drain(n):
        for _ in range(min(n, len(PENDING))):
            PENDING.pop(0)()

    def token_phase(i, final):
        S.fence()
        MODS, MA = MODSB[i % 2], MAB[i % 2]
        MODSN, MAN = MODSB[(i + 1) % 2], MAB[(i + 1) % 2]
        o_ = DYN
        ACC, o_ = alloc_at(o_, [128, DC, HT], F32)
        h2_off = o_
        H2, o_ = alloc_at(o_, [128, DC, HT], BF16)
        ZT, _ = alloc_at(h2_off, [128, 24 * 512], BF16)
        ZT2, o_ = alloc_at(o_, [128, 12 * 512], BF16)
        AB = []
        for _ in range(2):
            t, o_ = alloc_at(o_, [128, 4, 512], BF16)
            AB.append(t)
        RL, SQ, TMP = [], [], []
        for _ in range(2):
            t, o_ = alloc_at(o_, [128, 512], F32); RL.append(t)
            t, o_ = alloc_at(o_, [128, 512], BF16); SQ.append(t)
            t, o_ = alloc_at(o_, [128, 512], F32); TMP.append(t)
        RS, o_ = alloc_at(o_, [128, 512], F32)
        RSTD, o_ = alloc_at(o_, [128, 512], F32)
        allh2 = [("h2", c, t0) for c in range(DC) for (t0, _) in LT]
        ctr = [0]
        SS = lambda t: 2 if t == 0 else 0

        def rstd_of(t0, w):
            pb, pk = bank()
            for c in range(DC):
                q = ctr[0] % 2
                ctr[0] += 1
                act(SQ[q][:, :w], ACC[:, c, t0:t0 + w], AF.Square, r=[("acc", c, t0)], w=[("sq", q)])
                mm(pb(w), ones[:], SQ[q][:, :w], c == 0, c == DC - 1, r=[("sq", q), "ones"], w=[pk])
            act(RS[:, :w], pb(w), AF.Sqrt, bias=EPS, scale=1.0 / D, r=[pk], w=["rs"])
            dve(lambda e: e.reciprocal(RSTD[:, :w], RS[:, :w]), r=["rs"], w=["rstd"])

        def norm(t0, w, Aap, Bap, keys_r):
            rstd_of(t0, w)
            for c in range(DC):
                q = ctr[0] % 2
                ctr[0] += 1
                dve(lambda e, c=c, q=q: e.tensor_tensor(TMP[q][:, :w], ACC[:, c, t0:t0 + w], RSTD[:, :w], ALU.mult),
                    r=[("acc", c, t0), "rstd"], w=[("tmp", q)])
                act(H2[:, c, t0:t0 + w], TMP[q][:, :w], AF.Identity, bias=Bap(c), scale=Aap(c),
                    r=[("tmp", q)] + keys_r, w=[("h2", c, t0), "zt"])

        for t, (t0, t1) in enumerate(LT):
            src = (xT if i < 0 else Xm)[:, t0:t1].rearrange("(c p) n -> p c n", p=128)
            dma("sp", ACC[:, :, t0:t1], src, r=["xm"], w=[("acc", c, t0) for c in range(DC)])
        if i >= 0:
            lru = (i % 2 == 0)
            j = i // 2
            KZ = 2 * LCL if lru else DC
            KH = KZ // 2
            wsrc = lwout[j] if lru else nwo[j]
            for t, (t0, t1) in enumerate(LT):
                w = t1 - t0
                s_ = SS(t)
                zt = ZT[:, 0:KZ * 512].rearrange("p (k n) -> p k n", k=KZ)
                zt2 = ZT2[:, 0:KH * 512].rearrange("p (k n) -> p k n", k=KH)
                for rk in range(2):
                    ztr = zt[:, rk * KH:(rk + 1) * KH, :w]
                    for q in range(KH // 4):
                        dma("sp", zt[:, rk * KH + 4 * q: rk * KH + 4 * q + 4, :w],
                            Zg[0][q][rk * GQ:(rk + 1) * GQ, t0:t1].rearrange("(k p) n -> p k n", p=128),
                            r=[("zg", 0, q)], w=["zt"] + allh2)
                        dma("sp", zt2[:, 4 * q: 4 * q + 4, :w],
                            Zg[1][q][rk * GQ:(rk + 1) * GQ, t0:t1].rearrange("(k p) n -> p k n", p=128),
                            r=[("zg", 1, q)], w=["zt2"])
                    dve(lambda e, ztr=ztr: e.tensor_scalar(ztr, ztr, FLG[:, 0:1], None, ALU.mult), r=["zt", "flg"], w=["zt"])
                    dve(lambda e, ztr=ztr, zt2=zt2, w=w: e.scalar_tensor_tensor(ztr, zt2[:, :, :w], FLG[:, 1:2], ztr, ALU.mult, ALU.add),
                        r=["zt", "zt2", "flg"], w=["zt"])
                for o in range(DC):
                    slot = ring_next()
                    wt = RING[slot][:, 0:KZ * 128].rearrange("p (k n) -> p k n", k=KZ)
                    load_w(wt, wsrc[:, o * 128:(o + 1) * 128].rearrange("(k p) n -> p k n", p=128), slot)
                    pb, pk = bank()
                    for k in range(KZ):
                        mm(pb(w), wt[:, k, :], zt[:, k, :w], k == 0, k == KZ - 1, r=[("ring", slot), "zt"], w=[pk])
                    dve(lambda e, pb=pb, o=o, t0=t0, w=w, s_=s_: e.scalar_tensor_tensor(
                        ACC[:, o, t0:t0 + w], pb(w), MODS[:, 32 + o, s_:s_ + 1], ACC[:, o, t0:t0 + w], ALU.mult, ALU.add),
                        r=[pk, ("mods", i % 2), ("acc", o, t0)], w=[("acc", o, t0)])
            for t, (t0, t1) in enumerate(LT):
                s_ = SS(t)
                norm(t0, t1 - t0, lambda c, s_=s_: MA[:, 1, c, s_:s_ + 1], lambda c, s_=s_: MODS[:, 48 + c, s_:s_ + 1],
                     [("ma", i % 2), ("mods", i % 2)])
            for g in range(DFF // 512):
                sl1 = [ring_next(), ring_next()]
                w1t = []
                for u in range(2):
                    wt = RING[sl1[u]][:, :].rearrange("p (k n) -> p k n", k=DC)
                    load_w(wt, w1d[i, :, g * 512 + u * 256: g * 512 + (u + 1) * 256].rearrange("(k p) n -> p k n", p=128), sl1[u])
                    w1t.append(wt)
                sl2 = [ring_next(), ring_next()]
                w2t = []
                for u in range(2):
                    wt = RING[sl2[u]][:, :].rearrange("p (k n) -> p k n", k=2)
                    load_w(wt, w2d[i, g * 512 + u * 256: g * 512 + (u + 1) * 256, :].rearrange("(k p) n -> p k n", p=128), sl2[u])
                    w2t.append(wt)
                for t, (t0, t1) in enumerate(LT):
                    w = t1 - t0
                    s_ = SS(t)
                    ab = ctr[0] % 2
                    A = AB[ab]
                    for hc in range(4):
                        pb, pk = bank()
                        wt = w1t[hc // 2]
                        cs = (hc % 2) * 128
                        for k in range(DC):
                            mm(pb(w), wt[:, k, cs:cs + 128], H2[:, k, t0:t0 + w], k == 0, k == DC - 1,
                               r=[("ring", sl1[hc // 2]), ("h2", k, t0)], w=[pk])
                        q = ctr[0] % 2
                        ctr[0] += 1
                        act(RL[q][:, :w], pb(w), AF.Relu, r=[pk], w=[("rl", q)])
                        dve(lambda e, A=A, hc=hc, q=q, w=w: e.tensor_tensor(A[:, hc, :w], RL[q][:, :w], RL[q][:, :w], ALU.mult),
                            r=[("rl", q)], w=[("ab", ab, hc)])
                    if ctr[0] % 2 == ab:
                        ctr[0] += 1
                    for o in range(DC):
                        pb, pk = bank()
                        for hc in range(4):
                            mm(pb(w), w2t[hc // 2][:, hc % 2, o * 128:(o + 1) * 128], A[:, hc, :w], hc == 0, hc == 3,
                               r=[("ring", sl2[hc // 2]), ("ab", ab, hc)], w=[pk])
                        dve(lambda e, pb=pb, o=o, t0=t0, w=w, s_=s_: e.scalar_tensor_tensor(
                            ACC[:, o, t0:t0 + w], pb(w), MODS[:, 80 + o, s_:s_ + 1], ACC[:, o, t0:t0 + w], ALU.mult, ALU.add),
                            r=[pk, ("mods", i % 2), ("acc", o, t0)], w=[("acc", o, t0)])
        for t, (t0, t1) in enumerate(LT):
            w = t1 - t0
            s_ = SS(t)
            if final:
                rstd_of(t0, w)
                for c in range(DC):
                    dve(lambda e, c=c, t0=t0, w=w: e.tensor_tensor(ACC[:, c, t0:t0 + w], ACC[:, c, t0:t0 + w], RSTD[:, :w], ALU.mult),
                        r=[("acc", c, t0), "rstd"], w=[("acc", c, t0)])
                    act(ACC[:, c, t0:t0 + w], ACC[:, c, t0:t0 + w], AF.Identity, scale=FING[:, c:c + 1],
                        r=[("acc", c, t0), "fing"], w=[("acc", c, t0)])
                OUT_IDS.append(dma("sp", outT[:, t0:t1].rearrange("(c p) n -> p c n", p=128), ACC[:, :, t0:t1],
                                   r=[("acc", c, t0) for c in range(DC)], w=[("out", t)]))
            else:
                dma("sp", Xm[:, t0:t1].rearrange("(c p) n -> p c n", p=128), ACC[:, :, t0:t1],
                    r=[("acc", c, t0) for c in range(DC)], w=["xm"])
                norm(t0, w, lambda c, s_=s_: MAN[:, 0, c, s_:s_ + 1], lambda c, s_=s_: MODSN[:, c, s_:s_ + 1],
                     [("ma", (i + 1) % 2), ("mods", (i + 1) % 2)])
                dma("sp", Hs[:, t0:t1].rearrange("(c p) n -> p c n", p=128), H2[:, :, t0:t1],
                    r=[("h2", c, t0) for c in range(DC)], w=["hs"])
        if not final:
            for q in range(D // GQ):
                allgather(Hs[q * GQ:(q + 1) * GQ, :], Hg[q][:, :], r=["hs"], w=[("hg", q)])

    def load_H(H):
        for t, (t0, t1) in enumerate(TILES):
            pieces = []
            a = t0
            while a < t1:
                rk = a // HT
                b = min(t1, (rk + 1) * HT)
                pieces.append((a, b, rk))
                a = b
            for (a, b, rk) in pieces:
                for q in range(D // GQ):
                    dma("sp", H[:, 4 * q:4 * q + 4, a:b],
                        Hg[q][rk * GQ:(rk + 1) * GQ, a - rk * HT: b - rk * HT].rearrange("(c p) n -> p c n", p=128),
                        r=[("hg", q)], w=[("H", c, t) for c in range(4 * q, 4 * q + 4)])

    def store_Z(lc, src, c0, r):
        ids = []
        for h in range(2):
            a, b = max(c0, h * HT), (h + 1) * HT
            ids.append(dma("sp", Zs[h][lc * 128:(lc + 1) * 128, a - h * HT: b - h * HT], src[:, a:b], r=r, w=[("zs", h, lc // 4)]))
        return ids

    def gather_Zq(q):
        for h in range(2):
            allgather(Zs[h][q * GQ:(q + 1) * GQ, :], Zg[h][q][:, :], r=[("zs", h, q)], w=[("zg", h, q)])

    def lru_phase(i):
        j = i // 2
        S.fence()
        o_ = DYN
        H, o_ = alloc_at(o_, [128, DC, NTOK], BF16)
        UL = 2312
        U, o_ = alloc_at(o_, [128, 2, UL], F32)
        UCB, o_ = alloc_at(o_, [128, 2, NTOK], BF16)
        G, o_ = alloc_at(o_, [128, 2, NTOK], BF16)
        GT = []
        for _ in range(2):
            t, o_ = alloc_at(o_, [128, 512], F32)
            GT.append(t)
        Y = []
        for _ in range(2):
            t, o_ = alloc_at(o_, [128, NTOK], F32)
            Y.append(t)
        ZB1, o_ = alloc_at(o_, [128, NTOK], BF16)
        ZB = [ZB1, ZB1]
        NT = 2
        RB, IB, MB, HB_ = [], [], [], []
        for _ in range(NT):
            t, o_ = alloc_at(o_, [128, 512], F32); RB.append(t)
            t, o_ = alloc_at(o_, [128, 512], F32); IB.append(t)
            t, o_ = alloc_at(o_, [128, 512], F32); MB.append(t)
            t, o_ = alloc_at(o_, [128, 512], F32); HB_.append(t)
        load_H(H)
        dve(lambda e: e.memset(U[:, :, :], 0.0), w=[("U", 0), ("U", 1)])
        tctr = [0]
        for b in range(NBL):
            sg, su, sw = ring_next(), ring_next(), ring_next()
            wg = RING[sg][:, :].rearrange("p (k n) -> p k n", k=DC)
            wu = RING[su][:, :].rearrange("p (k n) -> p k n", k=DC)
            load_w(wg, lwin_g[j, :, b * 256:(b + 1) * 256].rearrange("(k p) n -> p k n", p=128), sg)
            load_w(wu, lwin_u[j, :, b * 256:(b + 1) * 256].rearrange("(k p) n -> p k n", p=128), su)
            wgt = RING[sw][:, 0:2048].rearrange("p (d g k n) -> p d g k n", d=2, g=2, k=2)
            for d in range(2):
                load_w(wgt[:, d, 0], lwa[j, d, b].rearrange("(k p) n -> p k n", p=128), sw)
                load_w(wgt[:, d, 1], lwx[j, d, b].rearrange("(k p) n -> p k n", p=128), sw)
            if b >= 2 and b % 2 == 0:
                gather_Zq((b - 2) // 2)
            for cc in range(2):
                ch = b * 2 + cc
                for t in range(5):
                    t0, t1 = TILES[t]
                    w = t1 - t0
                    pb, pk = bank()
                    for k in range(DC):
                        mm(pb(w), wg[:, k, cc * 128:(cc + 1) * 128], H[:, k, t0:t1], k == 0, k == DC - 1,
                           r=[("ring", sg), ("H", k, t)], w=[pk])
                    q = tctr[0] % 2
                    tctr[0] += 1
                    gt = GT[q]
                    act(gt[:, :w], pb(w), AF.Square, r=[pk], w=[("gt", q)])
                    dve(lambda e, gt=gt, w=w: e.tensor_scalar(gt[:, :w], gt[:, :w], 0.044715, 1.0, ALU.mult, ALU.add),
                        r=[("gt", q)], w=[("gt", q)])
                    dve(lambda e, gt=gt, w=w, pb=pb: e.tensor_tensor(gt[:, :w], gt[:, :w], pb(w), ALU.mult),
                        r=[("gt", q), pk], w=[("gt", q)])
                    act(gt[:, :w], gt[:, :w], AF.Tanh, scale=0.7978845608028654, r=[("gt", q)], w=[("gt", q)])
                    dve(lambda e, gt=gt, w=w, pb=pb, cc=cc, t0=t0, t1=t1: e.scalar_tensor_tensor(G[:, cc, t0:t1], gt[:, :w], 1.0, pb(w), ALU.add, ALU.mult),
                        r=[("gt", q), pk], w=[("G", cc, t)])
                for t in range(5):
                    t0, t1 = TILES[t]
                    w = t1 - t0
                    pb, pk = bank()
                    for k in range(DC):
                        mm(pb(w), wu[:, k, cc * 128:(cc + 1) * 128], H[:, k, t0:t1], k == 0, k == DC - 1,
                           r=[("ring", su), ("H", k, t)], w=[pk])
                    uo = 2 + t0 if t == 0 else 261 + (t0 - CTX)
                    act(U[:, cc, uo:uo + w], pb(w), AF.Copy, r=[pk], w=[("U", cc)])
                for (s0, n, d0) in [(0, CTX, 0), (259, SEQ, CTX)]:
                    dve(lambda e, cc=cc, ch=ch, s0=s0, n=n, d0=d0: e.tensor_scalar(
                        Y[cc][:, d0:d0 + n], U[:, cc, s0:s0 + n], LCW[:, j, ch, 0:1], LCB[:, j, ch:ch + 1], ALU.mult, ALU.add),
                        r=[("U", cc), "lcw", "lcb"], w=[("y", cc)])
                    for tap in range(1, 4):
                        last = tap == 3
                        dve(lambda e, cc=cc, ch=ch, s0=s0, n=n, d0=d0, tap=tap, last=last: e.scalar_tensor_tensor(
                            (UCB[:, cc, d0:d0 + n] if last else Y[cc][:, d0:d0 + n]), U[:, cc, s0 + tap:s0 + tap + n],
                            LCW[:, j, ch, tap:tap + 1], Y[cc][:, d0:d0 + n], ALU.mult, ALU.add),
                            r=[("U", cc), "lcw", ("y", cc)], w=[("ucb", cc)] if last else [("y", cc)])
            for cc in range(2):
                ch = b * 2 + cc
                y = Y[cc]
                for d in range(2):
                    order = [0, 1, 2, 3, 4] if d == 0 else [0, 4, 3, 2, 1]
                    prev = None
                    pq = None
                    groups = [order[0:2], order[2:4], order[4:5]]
                    si = 0
                    for grp in groups:
                        items = []
                        for t in grp:
                            q = tctr[0] % NT
                            tctr[0] += 1
                            items.append((t, q, si))
                            si += 1
                        for (t, q, _) in items:
                            t0, t1 = TILES[t]
                            w = t1 - t0
                            R_, I_, M_ = RB[q], IB[q], MB[q]
                            for gi, dst in enumerate([R_, I_]):
                                pb, pk = bank()
                                for k in range(2):
                                    mm(pb(w), wgt[:, d, gi, k, cc * 128:(cc + 1) * 128], UCB[:, k, t0:t1], k == 0, k == 1,
                                       r=[("ring", sw), ("ucb", 0), ("ucb", 1)], w=[pk])
                                bsrc = LBA if gi == 0 else LBX
                                act(dst[:, :w], pb(w), AF.Tanh, bias=bsrc[:, j, d, ch:ch + 1], scale=0.5, r=[pk, "lba", "lbx"], w=[("rim", q, gi)])
                            act(M_[:, :w], R_[:, :w], AF.Exp, bias=LSA[:, j, d, ch:ch + 1], scale=LSA[:, j, d, ch:ch + 1],
                                r=[("rim", q, 0), "lsa"], w=[("rim", q, 2)])
                            act(R_[:, :w], R_[:, :w], AF.Exp, bias=LSAH[:, j, d, ch:ch + 1], scale=LSAH[:, j, d, ch:ch + 1],
                                r=[("rim", q, 0), ("rim", q, 2), "lsah"], w=[("rim", q, 0)])
                        for (t, q, _) in items:
                            w = TILES[t][1] - TILES[t][0]
                            M_ = MB[q]
                            act(M_[:, :w], M_[:, :w], AF.Sqrt, bias=0.25, scale=-0.25, r=[("rim", q, 2)], w=[("rim", q, 2)])
                        for (t, q, si_) in items:
                            t0, t1 = TILES[t]
                            w = t1 - t0
                            R_, I_, M_, Hs_ = RB[q], IB[q], MB[q], HB_[q]
                            dve(lambda e, I_=I_, w=w, cc=cc, t0=t0, t1=t1: e.scalar_tensor_tensor(I_[:, :w], I_[:, :w], 1.0, UCB[:, cc, t0:t1], ALU.add, ALU.mult),
                                r=[("rim", q, 1), ("ucb", cc)], w=[("rim", q, 1)])
                            dve(lambda e, I_=I_, M_=M_, w=w: e.tensor_tensor(I_[:, :w], I_[:, :w], M_[:, :w], ALU.mult),
                                r=[("rim", q, 1), ("rim", q, 2)], w=[("rim", q, 1)])
                            if d == 0:
                                init = 0.0 if si_ == 0 else y[:, t0 - 1:t0]
                                dve(lambda e, y=y, R_=R_, I_=I_, w=w, t0=t0, t1=t1, init=init: e.tensor_tensor_scan(
                                    y[:, t0:t1], R_[:, :w], I_[:, :w], init, ALU.mult, ALU.add),
                                    r=[("rim", q, 0), ("rim", q, 1), ("y", cc)], w=[("y", cc)])
                            else:
                                init = 0.0 if si_ == 0 else prev[:, 0:1]
                                dve(lambda e, Hs_=Hs_, R_=R_, I_=I_, w=w, init=init: e.tensor_tensor_scan(
                                    Hs_[:, 0:w][:, ::-1], R_[:, 0:w][:, ::-1], I_[:, 0:w][:, ::-1], init, ALU.mult, ALU.add),
                                    r=[("rim", q, 0), ("rim", q, 1)] + ([("rim", pq, 3)] if prev is not None else []), w=[("rim", q, 3)])
                                dve(lambda e, y=y, Hs_=Hs_, w=w, t0=t0, t1=t1: e.tensor_tensor(y[:, t0:t1], y[:, t0:t1], Hs_[:, :w], ALU.add),
                                    r=[("rim", q, 3), ("y", cc)], w=[("y", cc)])
                                prev = Hs_
                                pq = q
                dve(lambda e, y=y, cc=cc: e.scalar_tensor_tensor(ZB[cc][:, :], y[:, :], 0.5, G[:, cc, :], ALU.mult, ALU.mult),
                    r=[("y", cc)] + [("G", cc, t) for t in range(5)], w=["zb"])
                store_Z(ch, ZB[cc], 0, ["zb"])
            drain(9)
        drain(1000)
        gather_Zq(NBL * 256 // GQ - 1)

    def na_phase(i):
        j = i // 2
        need_ctx = i < DEPTH - 1
        S.fence()
        o_ = DYN
        H, o_ = alloc_at(o_, [128, DC, NTOK], BF16)
        QT, o_ = alloc_at(o_, [128, NTOK], BF16)
        KT, o_ = alloc_at(o_, [128, NTOK], BF16)
        V, o_ = alloc_at(o_, [128, 18, 128], BF16)
        OT = []
        for _ in range(2):
            t, o_ = alloc_at(o_, [128, NTOK], BF16)
            OT.append(t)
        BI = []
        for _ in range(2):
            t, o_ = alloc_at(o_, [128, 5 * NKEY], F32)
            BI.append(t)
        SS_, PP, PT = [], [], []
        for _ in range(2):
            t, o_ = alloc_at(o_, [128, NKEY], F32); SS_.append(t)
            t, o_ = alloc_at(o_, [128, NKEY], BF16); PP.append(t)
            t, o_ = alloc_at(o_, [128, 7, 128], BF16); PT.append(t)
        ST, o_ = alloc_at(o_, [128, 2, 4], F32)
        load_H(H)
        qctr = [0]
        sc = 128 ** -0.5
        for hd in range(NHL):
            s1, s2 = ring_next(), ring_next()
            wq = RING[s1][:, 0:2048].rearrange("p (k n) -> p k n", k=DC)
            wk = RING[s1][:, 2048:4096].rearrange("p (k n) -> p k n", k=DC)
            wv = RING[s2][:, 0:2048].rearrange("p (k n) -> p k n", k=DC)
            load_w(wq, nwq[j, :, hd * 128:(hd + 1) * 128].rearrange("(k p) n -> p k n", p=128), s1)
            load_w(wk, nwk[j, :, hd * 128:(hd + 1) * 128].rearrange("(k p) n -> p k n", p=128), s1)
            load_w(wv, nwv[j, :, hd * 128:(hd + 1) * 128].rearrange("(k p) n -> p k n", p=128), s2)
            if hd == 4:
                gather_Zq(0)
            bi = BI[hd % 2]
            dma("sp", bi[:, :], nbias[j, hd], w=[("bi", hd % 2)])
            ot = OT[hd % 2]
            for t in range(5):
                t0, t1 = TILES[t]
                w = t1 - t0
                if t > 0 or need_ctx:
                    pb, pk = bank(3)
                    for k in range(DC):
                        mm(pb(w), wq[:, k, :], H[:, k, t0:t1], k == 0, k == DC - 1, r=[("ring", s1), ("H", k, t)], w=[pk])
                    act(QT[:, t0:t1], pb(w), AF.Copy, scale=sc, r=[pk], w=[("qt", t)])
                pb, pk = bank(3)
                for k in range(DC):
                    mm(pb(w), wk[:, k, :], H[:, k, t0:t1], k == 0, k == DC - 1, r=[("ring", s1), ("H", k, t)], w=[pk])
                dve(lambda e, pb=pb, w=w, t0=t0, t1=t1: e.tensor_copy(KT[:, t0:t1], pb(w)), r=[pk], w=[("kt", t)])
            for tc4 in range(0, 18, 4):
                n = min(4, 18 - tc4)
                pb, pk = bank(3)
                for u in range(n):
                    tc = tc4 + u
                    tt = [x for x in range(5) if TILES[x][0] <= tc * 128 < TILES[x][1]][0]
                    for k in range(DC):
                        mm(pb(512)[:, u * 128:(u + 1) * 128], H[:, k, tc * 128:(tc + 1) * 128], wv[:, k, :], k == 0, k == DC - 1,
                           r=[("ring", s2), ("H", k, tt)], w=[pk])
                act(V[:, tc4:tc4 + n, :], pb(n * 128).rearrange("p (a b) -> p a b", a=n), AF.Copy, r=[pk], w=[("v", x) for x in range(tc4, tc4 + n)])
            qtiles = ([("c", 0), ("c", 1)] if need_ctx else []) + [("l", x) for x in range(16)]
            infos = []
            for kind, jj in qtiles:
                q = qctr[0] % 2
                qctr[0] += 1
                inf = dict(kind=kind, q=q, sps=(sA if q == 0 else sB),
                           spk=([("ps", 4), ("ps", 5)] if q == 0 else [("ps", 6), ("ps", 7)]))
                if kind == "c":
                    inf.update(q0=jj * 128, nk=CTX, segs=[(0, 0, CTX)], qtk=("qt", 0), ktk=[("kt", 0)], k0=0, pat=0)
                else:
                    base = min(max(2 * jj - 4, 0), 22)
                    k0 = CTX + base * 64
                    inf.update(q0=CTX + jj * 128, nk=NKEY, segs=[(0, 0, 256), (256, k0, 256), (512, k0 + 256, 384)],
                               qtk=("qt", 1 + jj // 4), ktk=[("kt", x) for x in range(5)], k0=k0,
                               pat=(0 if jj == 0 else 1 if jj == 1 else 2 if jj <= 13 else 3 if jj == 14 else 4))
                infos.append(inf)

            def stage_A(f):
                for (so, ko, kw_) in f["segs"]:
                    mm(f["sps"][:, so:so + kw_], QT[:, f["q0"]:f["q0"] + 128], KT[:, ko:ko + kw_], True, True,
                       r=[f["qtk"]] + f["ktk"], w=f["spk"])

            def stage_B(f):
                q, nk, sps = f["q"], f["nk"], f["sps"]
                ss, pp = SS_[q], PP[q]
                bi_ = bi
                if f["kind"] == "c":
                    dve(lambda e: e.tensor_copy(ss[:, :nk], sps[:, :nk]), r=f["spk"], w=[("ss", q)])
                else:
                    pat = f["pat"]
                    dve(lambda e: e.tensor_tensor(ss[:, :], sps[:, 0:NKEY], bi_[:, pat * NKEY:(pat + 1) * NKEY], ALU.add),
                        r=f["spk"] + [("bi", hd % 2)], w=[("ss", q)])
                dve(lambda e: e.tensor_reduce(ST[:, q, 0:1], ss[:, :nk], AX.X, ALU.max), r=[("ss", q)], w=[("st", q, 0)])
                dve(lambda e: e.tensor_scalar(ST[:, q, 1:2], ST[:, q, 0:1], -1.0, None, ALU.mult), r=[("st", q, 0)], w=[("st", q, 1)])
                act(ss[:, :nk], ss[:, :nk], AF.Exp, bias=ST[:, q, 1:2], r=[("ss", q), ("st", q, 1)], w=[("ss", q), ("st", q, 2)],
                    accum=ST[:, q, 2:3])
                dve(lambda e: e.reciprocal(ST[:, q, 3:4], ST[:, q, 2:3]), r=[("st", q, 2)], w=[("st", q, 3)])
                act(pp[:, :nk], ss[:, :nk], AF.Identity, scale=ST[:, q, 3:4], r=[("ss", q), ("st", q, 3)], w=[("pp", q)])

            def stage_C(f):
                q, nk = f["q"], f["nk"]
                pp, pt = PP[q], PT[q]
                nkc = nk // 128
                for kc in range(nkc):
                    S.add("pe", lambda e, kc=kc: e.transpose(ptb[:, kc * 128:(kc + 1) * 128], pp[:, kc * 128:(kc + 1) * 128], ident[:]),
                          r=[("pp", q), "ident"], w=[("ps", 3)])
                act(pt[:, :nkc, :], ptb[:, :nkc * 128].rearrange("p (a b) -> p a b", a=nkc), AF.Copy, r=[("ps", 3)], w=[("pt", q)])
                pb, pk = bank(3)
                for kc in range(nkc):
                    vc = kc if (f["kind"] == "c" or kc < 2) else f["k0"] // 128 + (kc - 2)
                    mm(pb(128), V[:, vc, :], pt[:, kc, :], kc == 0, kc == nkc - 1, r=[("v", vc), ("pt", q)], w=[pk])
                q0 = f["q0"]
                ot_ = ot
                act(ot_[:, q0:q0 + 128], pb(128), AF.Identity, r=[pk], w=[("ot", hd % 2)])

            n_t = len(infos)
            if PIPE_NA:
                for step in range(n_t + 2):
                    if step < n_t:
                        stage_A(infos[step])
                    if 0 <= step - 1 < n_t:
                        stage_B(infos[step - 1])
                    if 0 <= step - 2 < n_t:
                        stage_C(infos[step - 2])
            else:
                for f_ in infos:
                    stage_A(f_)
                    stage_B(f_)
                    stage_C(f_)
            store_Z(hd, ot, 0 if need_ctx else CTX, [("ot", hd % 2)])
            drain(7)
        drain(1000)
        gather_Zq(NHL * 128 // GQ - 1)

    mods_phase(0)
    token_phase(-1, False)
    for i in range(n_layers):
        last = (i == n_layers - 1)
        if not last and OVERLAP_MODS:
            PENDING.extend(mods_steps(i + 1))
        if i % 2 == 0:
            lru_phase(i)
        else:
            na_phase(i)
        if not last and not OVERLAP_MODS:
            mods_phase(i + 1)
        token_phase(i, last)
    S.wait_all("sp", OUT_IDS)
    S.emit(nc)
    return nc, S


def _fm(v):
    v = np.asarray(v, np.float32)
    lead = v.shape[:-1]
    n = v.shape[-1] // 128
    v = v.reshape(lead + (n, 128))
    return np.ascontiguousarray(np.moveaxis(v, -1, 0))


def _na_bias(rpb):
    L = rpb.shape[0]
    out = np.zeros((L, NH, 128, 5, NKEY), np.float32)
    reps = [0, 1, 2, 14, 15]
    qi = np.arange(128)
    kk = np.arange(640)
    for p, jj in enumerate(reps):
        base = min(max(2 * jj - 4, 0), 22)
        qr = 2 * jj + qi // 64
        qc = qi % 64
        kr = base + kk // 64
        kc = kk % 64
        rs = np.clip(qr - 4, 0, 24)
        cs = np.clip(qc - 8, 0, 48)
        valid = ((kr[None, :] >= rs[:, None]) & (kr[None, :] < rs[:, None] + 8)
                 & (kc[None, :] >= cs[:, None]) & (kc[None, :] < cs[:, None] + 16))
        di = np.clip(kr[None, :] - qr[:, None] + 7, 0, 14)
        dj = np.clip(kc[None, :] - qc[:, None] + 15, 0, 30)
        g = rpb[:, :, di, dj]
        out[:, :, :, p, 256:] = np.where(valid[None, None], g, np.float32(NEG))
    return out.reshape(L, NH, 128, 5 * NKEY)


def _pad_last(a, n):
    if a.shape[-1] == n:
        return np.ascontiguousarray(a)
    out = np.zeros(a.shape[:-1] + (n,), a.dtype)
    out[..., :a.shape[-1]] = a
    return out


def _prep_inputs(inp):
    f = lambda k: np.ascontiguousarray(np.asarray(inp[k], np.float32))
    x, c, ctx, c_ctx = f("x"), f("c"), f("ctx"), f("c_ctx")
    WP = 2 * NBL * 256
    w_in = f("lru_w_in")
    w_g = _pad_last(w_in[:, :, :W_LRU], WP)
    w_u = _pad_last(w_in[:, :, W_LRU:], WP)
    cw = _pad_last(f("lru_conv_w"), WP)
    cb = _pad_last(f("lru_conv_b"), WP)
    lam = _pad_last(f("lru_lambda"), WP)
    ba = _pad_last(f("lru_ba").reshape(2, 2, W_LRU), WP)
    bx = _pad_last(f("lru_bx").reshape(2, 2, W_LRU), WP)
    wa = np.zeros((2, 2, 2 * NBL, 256, 256), np.float32); wa[:, :, :NBLK] = f("lru_wa")
    wx = np.zeros((2, 2, 2 * NBL, 256, 256), np.float32); wx[:, :, :NBLK] = f("lru_wx")
    w_out = np.zeros((2, WP, D), np.float32); w_out[:, :W_LRU] = f("lru_w_out")
    qkv = f("na_w_qkv")
    bias = _na_bias(f("na_rpb"))
    shared = {
        "ident": np.eye(128, dtype=np.float32),
        "ada_w": f("ada_w"),
        "ada_b": np.ascontiguousarray(_fm(f("ada_b")).reshape(128, DEPTH, 96)),
        "n1g": _fm(f("norm1_g")), "n2g": _fm(f("norm2_g")), "fing": _fm(f("final_g")),
        "mlp_w1": f("mlp_w1"), "mlp_w2": f("mlp_w2"),
        "lru_w_out": w_out, "na_w_o": f("na_w_o"),
    }
    per_rank = []
    for rk in range(2):
        cs = slice(rk * NBL * 256, (rk + 1) * NBL * 256)
        hs = slice(rk * NHL * 128, (rk + 1) * NHL * 128)
        per_rank.append({
            "lru_w_g": np.ascontiguousarray(w_g[:, :, cs]), "lru_w_u": np.ascontiguousarray(w_u[:, :, cs]),
            "lru_cw": np.ascontiguousarray(np.transpose(_fm(cw[:, :, cs]), (0, 1, 3, 2))),
            "lru_cb": _fm(cb[:, cs]), "lru_lam": _fm(lam[:, :, cs]),
            "lru_ba": _fm(ba[:, :, cs]), "lru_bx": _fm(bx[:, :, cs]),
            "lru_wa": np.ascontiguousarray(wa[:, :, rk * NBL:(rk + 1) * NBL]),
            "lru_wx": np.ascontiguousarray(wx[:, :, rk * NBL:(rk + 1) * NBL]),
            "na_w_q": np.ascontiguousarray(qkv[:, :, 0:D][:, :, hs]),
            "na_w_k": np.ascontiguousarray(qkv[:, :, D:2 * D][:, :, hs]),
            "na_w_v": np.ascontiguousarray(qkv[:, :, 2 * D:3 * D][:, :, hs]),
            "na_bias": np.ascontiguousarray(bias[:, rk * NHL:(rk + 1) * NHL]),
            "flags": np.ascontiguousarray(np.tile(np.array([[1.0 - rk, float(rk)]], np.float32), (128, 1))),
        })
    maps = []
    for core in range(8):
        b, rk = core // 2, core % 2
        m = dict(shared)
        m.update(per_rank[rk])
        full = np.concatenate([ctx[b], x[b]], axis=0)
        m["xT"] = np.ascontiguousarray(full[rk * HT:(rk + 1) * HT].T)
        m["cv"] = np.ascontiguousarray(np.stack([_fm(c[b]), _fm(c_ctx)], axis=-1))
        maps.append(m)
    return maps


_CACHE = {}


def kernel(**inputs):
    maps = _prep_inputs(inputs)
    if "nc" not in _CACHE:
        _CACHE["nc"] = build_program()[0]
    res = run_bass_kernel_spmd(_CACHE["nc"], maps, core_ids=list(range(8)))
    out = np.empty((4, SEQ, D), np.float32)
    for b in range(4):
        o0 = res.results[2 * b]["outT"]
        o1 = res.results[2 * b + 1]["outT"]
        out[b, :HT - CTX] = o0[:, CTX:].T
        out[b, HT - CTX:] = o1.T
    return out
```
